# Optimizing a Trainium2 kernel written in Bass

```python
import math
import jax, jax.numpy as jnp
from jax import lax
import numpy as np

D_MODEL = 1024
BATCH = 2
SEQ = 8192
DEPTH = 1
DEC_BATCH = 16
DEC_SEQ = 64
PAST_LEN = 4096

CHUNK = 64
HEAD_DIM = 64
SB_WIDTH = D_MODEL // 2
SB_HEADS = SB_WIDTH // HEAD_DIM
POOL_WIDTH = D_MODEL // 4
POOL_WINDOWS = (2, 4, 8, 16)
POOL_GROUPS = len(POOL_WINDOWS)
POOL_GROUP_DIM = POOL_WIDTH // POOL_GROUPS
POOL_STATE = max(POOL_WINDOWS) - 1
XA_WIDTH = D_MODEL // 4
XA_HEADS = 4
XA_HEAD_DIM = XA_WIDTH // XA_HEADS
N_MEM = 256
MIX_WIDTH = SB_WIDTH + POOL_WIDTH + XA_WIDTH
IN_SIZES = (SB_WIDTH, SB_WIDTH, SB_WIDTH, SB_WIDTH, POOL_WIDTH, POOL_WIDTH, XA_WIDTH, XA_WIDTH)
IN_WIDTH = sum(IN_SIZES)
IN_SPLITS = tuple(int(i) for i in np.cumsum(IN_SIZES)[:-1])
Q_BLOCK = 128
EPS = 1e-6

kernel_name = "stick_breaking_pool_memory_hybrid_step"


def rms_norm(x, g):
    xf = x.astype(jnp.float32)
    y = xf * lax.rsqrt(jnp.mean(xf * xf, axis=-1, keepdims=True) + EPS)
    return (y * g.astype(jnp.float32)).astype(x.dtype)


def _sb_block(q_blk, q_pos, k, v, k_pos):
    z = jnp.einsum('bqhd,bkhd->bhqk', q_blk, k).astype(jnp.float32) / math.sqrt(HEAD_DIM)
    causal = k_pos[None, :] < q_pos[:, None]
    log_beta = jax.nn.log_sigmoid(z)
    log_rest = jnp.where(causal, jax.nn.log_sigmoid(-z), 0.0)
    tail = lax.cumsum(log_rest, axis=3, reverse=True) - log_rest
    w = jnp.where(causal, jnp.exp(log_beta + tail), 0.0)
    return jnp.einsum('bhqk,bkhd->bqhd', w.astype(v.dtype), v)


def stick_breaking(q, k, v, q_pos, k_pos):
    B, T, H, Dh = q.shape
    if T <= Q_BLOCK:
        return _sb_block(q, q_pos, k, v, k_pos)
    nb = T // Q_BLOCK
    qb = q.reshape(B, nb, Q_BLOCK, H, Dh).transpose(1, 0, 2, 3, 4)
    pb = q_pos.reshape(nb, Q_BLOCK)
    ob = lax.map(lambda a: _sb_block(a[0], a[1], k, v, k_pos), (qb, pb))
    return ob.transpose(1, 0, 2, 3, 4).reshape(B, T, H, Dh)


def multiscale_pool(u, hist, start):
    B, T, C = u.shape
    up = jnp.concatenate([hist, u], axis=1).astype(jnp.float32)
    cs = jnp.concatenate([jnp.zeros((B, 1, C), jnp.float32), jnp.cumsum(up, axis=1)], axis=1)
    pos = start + jnp.arange(T)
    hi = cs[:, POOL_STATE + 1:POOL_STATE + 1 + T]
    means = []
    for g, w in enumerate(POOL_WINDOWS):
        sl = slice(g * POOL_GROUP_DIM, (g + 1) * POOL_GROUP_DIM)
        lo = cs[:, POOL_STATE + 1 - w:POOL_STATE + 1 - w + T, sl]
        cnt = jnp.minimum(pos + 1, w).astype(jnp.float32)
        means.append((hi[..., sl] - lo) / cnt[None, :, None])
    mean = jnp.concatenate(means, axis=-1)
    return (mean - u.astype(jnp.float32)).astype(u.dtype)


def memory_kv(mem, g_mem, w_mem_kv):
    B, N, _ = mem.shape
    kv = rms_norm(mem, g_mem) @ w_mem_kv
    mk, mv = jnp.split(kv, 2, axis=-1)
    return (mk.reshape(B, N, XA_HEADS, XA_HEAD_DIM), mv.reshape(B, N, XA_HEADS, XA_HEAD_DIM))


def memory_attend(q, mk, mv):
    s = jnp.einsum('bthd,bnhd->bhtn', q, mk).astype(jnp.float32) / math.sqrt(XA_HEAD_DIM)
    p = jax.nn.softmax(s, axis=-1)
    return jnp.einsum('bhtn,bnhd->bthd', p.astype(mv.dtype), mv)


def mixer_layer(x, start, k_past, v_past, pool_hist, mk, mv, g_norm, w_in, pool_w, pool_scale, w_out):
    B, T, _ = x.shape
    h = rms_norm(x, g_norm)
    z = h @ w_in
    q_sb, k_sb, v_sb, g_sb, u_pool, g_pool, q_xa, g_xa = jnp.split(z, IN_SPLITS, axis=-1)
    q_sb = q_sb.reshape(B, T, SB_HEADS, HEAD_DIM)
    k_new = k_sb.reshape(B, T, SB_HEADS, HEAD_DIM)
    v_new = v_sb.reshape(B, T, SB_HEADS, HEAD_DIM)
    if k_past is None:
        k_all, v_all = k_new, v_new
    else:
        k_all = jnp.concatenate([k_past, k_new], axis=1)
        v_all = jnp.concatenate([v_past, v_new], axis=1)
    q_pos = start + jnp.arange(T)
    k_pos = jnp.arange(k_all.shape[1])
    o_sb = stick_breaking(q_sb, k_all, v_all, q_pos, k_pos).reshape(B, T, SB_WIDTH)
    pooled = multiscale_pool(u_pool, pool_hist, start).reshape(B, T, POOL_GROUPS, POOL_GROUP_DIM)
    o_pool = jnp.einsum('btgc,gcd->btgd', pooled, pool_w).reshape(B, T, POOL_WIDTH) * pool_scale
    new_hist = jnp.concatenate([pool_hist, u_pool], axis=1)[:, -POOL_STATE:]
    o_xa = memory_attend(q_xa.reshape(B, T, XA_HEADS, XA_HEAD_DIM), mk, mv).reshape(B, T, XA_WIDTH)
    mixed = jnp.concatenate([o_sb * jax.nn.silu(g_sb), o_pool * jax.nn.silu(g_pool), o_xa * jax.nn.silu(g_xa)], axis=-1)
    return x + mixed @ w_out, k_new, v_new, new_hist


def setup_inputs(seed: int = 0) -> dict:
    key = jax.random.key(seed)
    ks = jax.random.split(key, 17)

    def nrm(k, shape, s=1.0):
        return s * jax.random.normal(k, shape, jnp.float32)

    return {
        "x_prompt": nrm(ks[0], (BATCH, SEQ, D_MODEL)),
        "x_sample": nrm(ks[1], (DEC_BATCH, DEC_SEQ, D_MODEL)),
        "cache_sb_k": nrm(ks[2], (DEPTH, DEC_BATCH, PAST_LEN, SB_HEADS, HEAD_DIM)),
        "cache_sb_v": nrm(ks[3], (DEPTH, DEC_BATCH, PAST_LEN, SB_HEADS, HEAD_DIM)),
        "state_pool": nrm(ks[4], (DEPTH, DEC_BATCH, POOL_STATE, POOL_WIDTH)),
        "cache_mem_k": nrm(ks[5], (DEPTH, DEC_BATCH, N_MEM, XA_HEADS, XA_HEAD_DIM)),
        "cache_mem_v": nrm(ks[6], (DEPTH, DEC_BATCH, N_MEM, XA_HEADS, XA_HEAD_DIM)),
        "mem_prompt": nrm(ks[7], (BATCH, N_MEM, D_MODEL)),
        "g_norm": 1.0 + nrm(ks[8], (DEPTH, D_MODEL), 0.01),
        "w_in": nrm(ks[9], (DEPTH, D_MODEL, IN_WIDTH), D_MODEL ** -0.5),
        "pool_w": nrm(ks[10], (DEPTH, POOL_GROUPS, POOL_GROUP_DIM, POOL_GROUP_DIM), POOL_GROUP_DIM ** -0.5),
        "pool_scale": 1.0 + nrm(ks[11], (DEPTH, POOL_WIDTH), 0.1),
        "g_mem": 1.0 + nrm(ks[12], (DEPTH, D_MODEL), 0.01),
        "w_mem_kv": nrm(ks[13], (DEPTH, D_MODEL, 2 * XA_WIDTH), D_MODEL ** -0.5),
        "w_out": nrm(ks[14], (DEPTH, MIX_WIDTH, D_MODEL), MIX_WIDTH ** -0.5),
        "g_final": 1.0 + nrm(ks[15], (D_MODEL,), 0.01),
    }


def reference(x_prompt, x_sample, cache_sb_k, cache_sb_v, state_pool, cache_mem_k, cache_mem_v, mem_prompt,
              g_norm, w_in, pool_w, pool_scale, g_mem, w_mem_kv, w_out, g_final):
    past = cache_sb_k.shape[2]
    yp, ys = x_prompt, x_sample
    kp_l, vp_l, hp_l, mkp_l, mvp_l, ks_l, vs_l, hs_l = [], [], [], [], [], [], [], []
    for l in range(DEPTH):
        mk, mv = memory_kv(mem_prompt, g_mem[l], w_mem_kv[l])
        hist0 = jnp.zeros((x_prompt.shape[0], POOL_STATE, POOL_WIDTH), x_prompt.dtype)
        yp, kp, vp, hp = mixer_layer(yp, 0, None, None, hist0, mk, mv,
                                     g_norm[l], w_in[l], pool_w[l], pool_scale[l], w_out[l])
        ys, kn, vn, hn = mixer_layer(ys, past, cache_sb_k[l], cache_sb_v[l], state_pool[l],
                                     cache_mem_k[l], cache_mem_v[l],
                                     g_norm[l], w_in[l], pool_w[l], pool_scale[l], w_out[l])
        kp_l.append(kp); vp_l.append(vp); hp_l.append(hp); mkp_l.append(mk); mvp_l.append(mv)
        ks_l.append(kn); vs_l.append(vn); hs_l.append(hn)
    y_prompt = rms_norm(yp, g_final)
    y_sample = rms_norm(ys, g_final)
    sb_k_prompt = jnp.stack(kp_l, axis=0)
    sb_v_prompt = jnp.stack(vp_l, axis=0)
    pool_prompt = jnp.stack(hp_l, axis=0)
    mem_k_prompt = jnp.stack(mkp_l, axis=0)
    mem_v_prompt = jnp.stack(mvp_l, axis=0)
    sb_k_sample = jnp.stack(ks_l, axis=0)
    sb_v_sample = jnp.stack(vs_l, axis=0)
    pool_sample = jnp.stack(hs_l, axis=0)
    return (y_prompt, y_sample, sb_k_prompt, sb_v_prompt, pool_prompt, mem_k_prompt, mem_v_prompt, sb_k_sample, sb_v_sample, pool_sample)
```

```python
import contextlib
import numpy as np
import concourse.bass as bass
import concourse.mybir as mybir
from concourse.bass_utils import run_bass_kernel_spmd

F32 = mybir.dt.float32
BF16 = mybir.dt.bfloat16
AF = mybir.ActivationFunctionType
ALU = mybir.AluOpType

NCORES = 8
D = 1024
SEQ = 8192
NT = 16
TW = 512
PAST = 4096
EPS = 1e-6
NEG = -30000.0
NTOK = SEQ + 512
OUTTOK = 2048 + 128

import os
DEBUG_TILES = int(os.environ["KDBG_TILES"]) if "KDBG_TILES" in os.environ else None
DEBUG_NO_P4 = "KDBG_NOP4" in os.environ
DEBUG_NO_SAMPLE = "KDBG_NOSAMPLE" in os.environ
DEBUG_DUMP = "KDBG_DUMP" in os.environ
DEBUG_S = os.environ.get("KDBG_S", "")


class Tok:
    __slots__ = ("eng", "sem", "inc", "value", "used")

    def __init__(self, eng, sem, inc):
        self.eng, self.sem, self.inc, self.value, self.used = eng, sem, inc, None, False


class Prog:
    ENGS = ("pe", "act", "dve", "pool", "sp")

    def __init__(self):
        self.ops = {e: [] for e in self.ENGS}
        self.writer = {}
        self.readers = {}
        self.final = []
        self.defer = None

    def begin_defer(self):
        self.defer = []

    def end_defer(self):
        d, self.defer = self.defer, None
        return d

    def run(self, lst, k):
        n = 0
        while lst and n < k:
            a = lst.pop(0)
            if a[0] == "__group__":
                self.group(a[1], self._last_tok)
            else:
                self._last_tok = self.add(*a[0], **a[1])
            n += 1

    def plan_bg(self, lst):
        nodes = []
        writer, readers = {}, {}
        for it in lst:
            if it[0] == "__group__":
                nodes[-1]["grp"] = it[1]
                for k in it[1]:
                    writer[k] = len(nodes) - 1
                    readers[k] = set()
                continue
            (eng, fn), kw = it
            deps = set()
            for k in list(kw["r"]) + list(kw["rw"]):
                if k in writer:
                    deps.add(writer[k])
            for k in list(kw["w"]) + list(kw["rw"]):
                if k in writer:
                    deps.add(writer[k])
                deps |= readers.get(k, set())
            idx = len(nodes)
            for k in kw["r"]:
                readers.setdefault(k, set()).add(idx)
            for k in list(kw["w"]) + list(kw["rw"]):
                writer[k] = idx
                readers[k] = set()
            deps.discard(idx)
            nodes.append(dict(item=it, deps=deps, eng=eng, dma=kw["dsem"] is not None, grp=None, issued=None))
        return nodes

    def issue_bg(self, nodes, n, caps, flush=False):
        used = {}
        for nd in nodes:
            if nd["issued"] is not None:
                continue
            ok = True
            for d in nd["deps"]:
                dn = nodes[d]
                if dn["issued"] is None:
                    ok = False
                    break
                if dn["eng"] == nd["eng"] and not dn["dma"]:
                    lag = 0
                elif nd["eng"] in ("act", "pe"):
                    lag = 4 if dn["dma"] else (3 if dn["eng"] == "pool" else 1)
                else:
                    lag = 3 if dn["dma"] else 0
                if not flush and dn["issued"] + lag > n:
                    ok = False
                    break
            if not ok:
                continue
            e = nd["eng"]
            if not flush and used.get(e, 0) >= caps.get(e, 1):
                continue
            self.add(*nd["item"][0], **nd["item"][1])
            if nd["grp"]:
                self.group(nd["grp"], self._last_tok)
            nd["issued"] = n
            used[e] = used.get(e, 0) + 1

    def add(self, eng, fn, r=(), w=(), rw=(), dsem=None, extra=(), final=False, inc=None):
        if self.defer is not None:
            self.defer.append(((eng, fn), dict(r=r, w=w, rw=rw, dsem=dsem, extra=extra, final=final, inc=inc)))
            return None
        sem = dsem if dsem is not None else "p_" + eng
        tok = Tok(eng, sem, inc if inc is not None else (16 if dsem is not None else 1))
        deps = []
        for k in list(r) + list(rw):
            t = self.writer.get(k)
            if t is not None:
                deps.append(t)
        for k in list(w) + list(rw):
            t = self.writer.get(k)
            if t is not None:
                deps.append(t)
            deps.extend(self.readers.get(k, {}).values())
        deps.extend(t for t in extra if t is not None)
        for k in r:
            self.readers.setdefault(k, {})[sem] = tok
        for k in list(w) + list(rw):
            self.writer[k] = tok
            self.readers[k] = {}
        deps = [d for d in deps if not (d.eng == "pe" and eng == "pe")]
        for d in deps:
            d.used = True
        if dsem is not None:
            tok.used = True
        if final:
            tok.used = True
            self.final.append(tok)
        self.ops[eng].append((fn, deps, tok))
        self._last_tok = tok
        return tok

    def group_last(self, keys):
        if self.defer is not None:
            self.defer.append(("__group__", list(keys)))
        else:
            self.group(keys, self._last_tok)

    def group(self, keys, tok):
        for k in keys:
            self.writer[k] = tok
            self.readers[k] = {}

    def finalize(self):
        counts = {}
        for e in self.ENGS:
            for fn, deps, tok in self.ops[e]:
                if tok.used:
                    counts[tok.sem] = counts.get(tok.sem, 0) + tok.inc
                    tok.value = counts[tok.sem]
        return sorted(counts.keys())

    def emit(self, eng_name, e, sems, final_waits=False):
        waited = {}
        for fn, deps, tok in self.ops[eng_name]:
            need = {}
            for d in deps:
                if need.get(d.sem, 0) < d.value:
                    need[d.sem] = d.value
            for s, v in need.items():
                if waited.get(s, 0) < v:
                    e.wait_ge(sems[s], v)
                    waited[s] = v
            ins = fn(e)
            if tok.used:
                ins.then_inc(sems[tok.sem], tok.inc)
        if final_waits:
            need = {}
            for t in self.final:
                if need.get(t.sem, 0) < t.value:
                    need[t.sem] = t.value
            for s, v in need.items():
                e.wait_ge(sems[s], v)


def build_nc():
    nc = bass.Bass("TRN2", target_bir_lowering=False)
    P = Prog()

    def din(name, shape, dt=F32):
        return nc.dram_tensor(name, list(shape), dt, kind="ExternalInput").ap()

    def dout(name, shape, dt=F32):
        return nc.dram_tensor(name, list(shape), dt, kind="ExternalOutput").ap()

    xT = din("xT", [D, NTOK])
    wsel = din("wsel", [D, 768])
    wmsel = din("wmsel", [D, 128])
    memT = din("memT", [D, 256])
    gn = din("gn", [128, 8])
    gm = din("gm", [128, 8])
    poolw = din("poolw", [64, 64])
    pscale = din("pscale", [64, 1])
    selw = din("selw", [64, 4])
    corr = din("corr", [64, 512])
    wout = din("wout", [D, D])
    gfin = din("gfin", [128, D])
    xres = din("xres", [OUTTOK, D])
    kcT = din("kcT", [16, 128, 8 * 256])
    vc = din("vc", [16, 128, 2 * 8 * 128])
    hist = din("hist", [64, 8, 15])
    mkcT = din("mkcT", [64, 8, 256])
    mvc = din("mvc", [256, 8, 128])
    cst = din("cst", [128, 5, 128])
    cmask = din("cmask", [128, 128])
    cmask_s = din("cmask_s", [128, 512])

    k_out = dout("k_out", [NTOK, 128])
    v_out = dout("v_out", [NTOK, 128])
    u_out = dout("u_out", [64, 135])
    memkv_out = dout("memkv_out", [256, 128])
    y_out = dout("y_out", [OUTTOK, D])
    if DEBUG_DUMP:
        dbg = dout("dbg", [17, 4, 128, 512])
        dbgM = dout("dbgM", [17, 2, 128, 512], BF16)
        dbgMG = dout("dbgMG", [128, 8, 2176], BF16)
        dbgY1 = dout("dbgY1", [OUTTOK, D])

    cc_in = nc.dram_tensor("cc_in", [4 * 256, 2048], BF16, kind="Internal").ap()
    cc_out = nc.dram_tensor("cc_out", [4 * 4 * 256, 2048], BF16, kind="Internal").ap()
    cc_in_s = nc.dram_tensor("cc_in_s", [4 * 256, 128], BF16, kind="Internal").ap()
    cc_out_s = nc.dram_tensor("cc_out_s", [4 * 4 * 256, 128], BF16, kind="Internal").ap()
    RG = [[0, 1, 2, 3], [4, 5, 6, 7]]

    es = contextlib.ExitStack()
    with es:
        def sb(name, shape, dt):
            return es.enter_context(nc.sbuf_tensor(name, list(shape), dt))

        def ps(name):
            return es.enter_context(nc.psum_tensor(name, [128, 512], F32))

        ARENA_W = 16384
        arena = sb("arena", [128, ARENA_W], F32)

        def carve(off_bytes, nbytes, dt, pattern=None, **kw):
            a = arena[:, off_bytes // 4:(off_bytes + nbytes) // 4]
            if dt != F32:
                a = a.bitcast(dt)
            if pattern is not None:
                a = a.rearrange(pattern, **kw)
            return a

        Wg = sb("Wg", [128, 8, 768], BF16)
        WO = sb("WO", [128, 8, 1024], BF16)
        tri_bf = sb("tri_bf", [128, 128], BF16)
        negones_bf = sb("negones_bf", [128, 128], BF16)
        ident_bf = sb("ident_bf", [128, 128], BF16)
        ones_bf = sb("ones_bf", [128, 128], BF16)
        onespad_bf = sb("onespad_bf", [128, 128], BF16)
        cmask_bf = sb("cmask_bf", [128, 128], BF16)
        cmask_s_bf = sb("cmask_s_bf", [128, 512], BF16)
        one32 = sb("one32", [128, 1], F32)
        gn_t = sb("gn_t", [128, 8], F32)
        gm_t = sb("gm_t", [128, 8], F32)
        selw_t = sb("selw_t", [128, 4], F32)
        pscale_t = sb("pscale_t", [128, 1], F32)
        corr_t = sb("corr_t", [128, 512], F32)
        poolw32 = sb("poolw32", [128, 64], F32)
        poolw_bf = sb("poolw_bf", [128, 64], BF16)
        mkT_bf = sb("mkT_bf", [128, 256], BF16)
        mvpad_bf = sb("mvpad_bf", [128, 2, 128], BF16)
        xb = sb("xb", [128, 8, 512], BF16)
        xsq = sb("xsq", [128, 8, 512], BF16)
        lnms = sb("lnms", [128, 512], F32)
        rstd_bc = sb("rstd_bc", [128, 512], F32)
        rstd_tm = sb("rstd_tm", [128, 8], F32)
        qT2 = sb("qT2", [128, 2, 512], BF16)
        sg1_2 = sb("sg1_2", [128, 2, 512], F32)
        sg2 = sb("sg2", [128, 512], F32)
        tmpa = sb("tmpa", [128, 512], F32)
        tmpb = sb("tmpb", [128, 512], F32)
        qx = sb("qx", [128, 512], BF16)
        kv32 = sb("kv32", [128, 8, 256], F32)
        ubuf = sb("ubuf", [128, 640], F32)
        sa = sb("sa", [128, 640], F32)
        sbb = sb("sbb", [128, 640], F32)
        pacc = sb("pacc", [128, 512], F32)
        pooled = sb("pooled", [128, 512], BF16)
        pT = sb("pT", [128, 2, 512], BF16)
        rden = sb("rden", [128, 512], F32)
        O2 = sb("O2", [128, 512], F32)
        Ebuf = sb("Ebuf", [128, 2, 1024], F32)
        Lp = sb("Lp", [128, 2, 1024], BF16)
        Wbuf = sb("Wbuf", [128, 2, 1024], BF16)
        Sbf = sb("Sbf", [128, 2, 1024], BF16)
        Mt = sb("Mt", [128, 2, 2, 512], BF16)

        ZZ = [es.enter_context(nc.psum_tensor("ZZ%d" % i, [128, 1024], F32)) for i in range(2)]
        ACC = [ps("ACC%d" % i) for i in range(2)]
        PB = [ps("PB%d" % i) for i in range(2)]

        wst = carve(16384, 24576, F32, "p (c n) -> p c n", c=8)
        wmst = carve(40960, 4096, F32, "p (c n) -> p c n", c=8)
        cst_st = carve(45056, 2560, F32, "p (c n) -> p c n", c=5)
        cmask_st = carve(47616, 512, F32)
        cmask_s_st = carve(48128, 2048, F32)
        memxs = carve(50176, 8192, F32, "p (c n) -> p c n", c=8)
        Wmt = carve(58368, 2048, BF16, "p (c n) -> p c n", c=8)
        Wmf = carve(60416, 2048, BF16, "p (c n) -> p c n", c=8)
        xs = carve(0, 16384, F32, "p (c n) -> p c n", c=8)
        kT_all = carve(16384, 17408, BF16)
        V_all = carve(33792, 16384, BF16, "p (k n) -> p k n", k=64)
        kbf = [carve(34816 + 4096 * i, 4096, BF16, "p (b n) -> p b n", b=8) for i in range(2)]
        vbf = [carve(43008 + 4096 * i, 4096, BF16, "p (l b n) -> p l b n", l=2, b=8) for i in range(2)]
        cslots = sb("cslots", [128, 4, 2048], BF16)
        kbf += [cslots[:, i, :].rearrange("p (b n) -> p b n", b=8) for i in range(2)]
        vbf += [cslots[:, 2 + i, :].rearrange("p (l b n) -> p l b n", l=2, b=8) for i in range(2)]
        kTs_bf = carve(61440, 1024, BF16)
        Vs_bf = carve(51200, 2048, BF16, "p (b n) -> p b n", b=8)
        mkc_bf = carve(53248, 4096, BF16, "p (b n) -> p b n", b=8)
        mvc_bf = carve(57344, 4096, BF16, "p (m b n) -> p m b n", m=2, b=8)
        MG = carve(0, 34816, BF16, "p (c n) -> p c n", c=8)
        gfin_t = xb[:, 0:4, :].bitcast(F32).rearrange("p c n -> p (c n)")
        xr = [kv32[:, 4 * i:4 * i + 4, :].rearrange("p c n -> p (c n)") for i in range(2)]
        y1s = [carve(34816 + 4096 * i, 4096, F32) for i in range(2)]
        yo = [carve(43008 + 4096 * i, 4096, F32) for i in range(2)]
        ssq_t = sb("ssq_t", [128, 4], F32)
        MGs = sb("MGs", [128, 8, 128], BF16)

        def mm(out, lhsT, rhs, start, stop):
            return lambda e: e.matmul(out, lhsT, rhs, start=start, stop=stop, skip_group_check=True)

        def actf(out, in_, func, scale=1.0, bias=0.0):
            return lambda e: e.activation(out=out, in_=in_, func=func, scale=scale, bias=bias)

        def tt(out, a, b, op):
            return lambda e: e.tensor_tensor(out=out, in0=a, in1=b, op=op)

        def tsm(out, a, s):
            return lambda e: e.tensor_scalar(out=out, in0=a, scalar1=s, scalar2=None, op0=ALU.mult)

        def stt(out, a, s, b, op0, op1):
            return lambda e: e.scalar_tensor_tensor(out=out, in0=a, scalar=s, in1=b, op0=op0, op1=op1)

        def cp(out, in_):
            return lambda e: e.tensor_copy(out, in_)

        def mset(ap, v):
            return lambda e: e.memset(ap, v)

        def dma(out, in_):
            return lambda e: e.dma_start(out=out, in_=in_)

        P.add("sp", dma(cst_st, cst), w=["cst_st"], dsem="d_c")
        P.add("sp", dma(cmask_st, cmask), w=["cmask_st"], dsem="d_c")
        P.add("sp", dma(cmask_s_st, cmask_s), w=["cmask_s_st"], dsem="d_c")
        P.add("sp", dma(gn_t[:], gn), w=["gn"], dsem="d_c")
        P.add("sp", dma(gm_t[:], gm), w=["gm"], dsem="d_c")
        P.add("sp", dma(selw_t[0:64, :], selw), w=["selw"], dsem="d_c")
        P.add("sp", dma(pscale_t[0:64, :], pscale), w=["pscale"], dsem="d_c")
        P.add("sp", dma(corr_t[0:64, :], corr), w=["corr"], dsem="d_c")
        tlast = P.add("sp", dma(poolw32[0:64, :], poolw), w=["poolw32"], dsem="d_c")
        P.group(["cst_st", "cmask_st", "cmask_s_st", "gn", "gm", "selw", "pscale", "corr", "poolw32"], tlast)
        P.add("sp", dma(wmst, wmsel.rearrange("(c p) n -> p c n", p=128)), w=["wmst"], dsem="d_w")
        tlast = P.add("sp", dma(memxs, memT.rearrange("(c p) n -> p c n", p=128)), w=["memxs"], dsem="d_w")
        P.group(["wmst", "memxs"], tlast)
        for hh in range(2):
            tlast = P.add("sp", dma(wst[:, 4 * hh:4 * hh + 4, :],
                                    wsel.rearrange("(c p) n -> p c n", p=128)[:, 4 * hh:4 * hh + 4, :]),
                          w=["wst%d" % hh], dsem="d_w2")
        P.group(["wst0", "wst1"], tlast)

        for i, t in enumerate([tri_bf, negones_bf, ident_bf, ones_bf, onespad_bf]):
            P.add("dve", cp(t[:], cst_st[:, i, :]), r=["cst_st"], w=["const%d" % i])
        P.add("dve", cp(cmask_bf[:], cmask_st), r=["cmask_st"], w=["cmask_bf"])
        P.add("dve", cp(cmask_s_bf[:], cmask_s_st), r=["cmask_s_st"], w=["cmask_s_bf"])
        P.add("dve", mset(one32[:], 1.0), w=["one32"])
        P.add("dve", cp(poolw_bf[0:64, :], poolw32[0:64, :]), r=["poolw32"], w=["poolw_bf"])
        WG = ["Wg%d" % c for c in range(8)]
        P.add("dve", mset(Wmf, 0.0), w=["Wmf"])
        for c in range(8):
            P.add("dve", tsm(Wmt[:, c, :], wmst[:, c, :], gm_t[:, c:c + 1]), r=["wmst", "gm"], rw=["Wmt"])
        P.add("dve", cp(Wmf[:, :, 64:128], Wmt[:, :, 0:64]), r=["Wmt"], rw=["Wmf"])
        P.add("dve", mset(mvpad_bf[:], 0.0), w=["mvpad"])
        for i in range(2):
            P.add("pool", dma(WO[:, 4 * i:4 * i + 4, :], wout.rearrange("(c p) n -> p c n", p=128)[:, 4 * i:4 * i + 4, :]),
                  rw=["WO"], dsem="d_wo")
        P.group_last(["WO"])

        pbi = [0]

        def next_pb():
            i = pbi[0]
            pbi[0] ^= 1
            return i

        def rms_stats(src, srckeys, n, psub, nsub):
            for c in range(8):
                P.add("dve", cp(xb[:, c, 0:n], src[:, c, 0:n]), r=srckeys, w=["xb%d" % c])
            b = next_pb()
            for c in range(8):
                P.add("dve", tt(xsq[:, c, 0:n], src[:, c, 0:n], src[:, c, 0:n], ALU.mult),
                      r=srckeys, w=["xsq%d" % c])
                P.add("pe", mm(PB[b][:, 0:n], ones_bf[:], xsq[:, c, 0:n], c == 0, c == 7),
                      r=["xsq%d" % c, "const3"], rw=["PB%d" % b])
            P.add("act", actf(lnms[:, 0:n], PB[b][:, 0:n], AF.Ln, scale=1.0 / D, bias=EPS),
                  r=["PB%d" % b], w=["lnms"])
            P.add("act", actf(rstd_bc[:, 0:n], lnms[:, 0:n], AF.Exp, scale=-0.5), r=["lnms"], w=["rstd_bc"])
            b = next_pb()
            for s in range(nsub):
                P.add("pe", mm(PB[b][0:psub, s:s + 1], rstd_bc[0:1, s * psub:(s + 1) * psub], one32[0:1, 0:1],
                               s == 0, s == nsub - 1),
                      r=["rstd_bc", "one32"], rw=["PB%d" % b])
            P.add("dve", cp(rstd_tm[0:psub, 0:nsub], PB[b][0:psub, 0:nsub]), r=["PB%d" % b], w=["rstd_tm"])

        XB = ["xb%d" % c for c in range(8)]

        rms_stats(memxs, ["memxs"], 256, 128, 2)
        b = next_pb()
        for c in range(8):
            P.add("pe", mm(PB[b][:, 0:256], Wmf[:, c, :], xb[:, c, 0:256], c == 0, c == 7),
                  r=["Wmf", "xb%d" % c], rw=["PB%d" % b])
        P.add("dve", tt(mkT_bf[64:128, :], PB[b][64:128, 0:256], rstd_bc[64:128, 0:256], ALU.mult),
              r=["PB%d" % b, "rstd_bc"], w=["mkT_bf"])
        b = next_pb()
        for m in range(2):
            for c in range(8):
                P.add("pe", mm(PB[b][:, m * 128:(m + 1) * 128], xb[:, c, m * 128:(m + 1) * 128], Wmt[:, c, :],
                               c == 0, c == 7),
                      r=["Wmt", "xb%d" % c], rw=["PB%d" % b])
        for m in range(2):
            P.add("dve", tsm(kv32[:, m, 0:128], PB[b][:, m * 128:(m + 1) * 128], rstd_tm[:, m:m + 1]),
                  r=["PB%d" % b, "rstd_tm"], rw=["kv32"])
        P.add("dve", cp(mvpad_bf[:, :, 64:128], kv32[:, 0:2, 64:128]), r=["kv32"], rw=["mvpad"])
        P.add("pool", dma(memkv_out.rearrange("(m p) f -> p m f", p=128), kv32[:, 0:2, 0:128]),
              r=["kv32"], dsem="o_misc", final=True)

        for c in range(8):
            P.add("dve", tsm(Wg[:, c, :], wst[:, c, :], gn_t[:, c:c + 1]),
                  r=["wst%d" % (c // 4), "gn"], w=["Wg%d" % c])

        state = {"blk": 0}
        PROMPT_ARENA_KEYS = ["kT%d" % i for i in range(NT)] + ["V%d" % i for i in range(NT)]

        SETUP_ARENA_KEYS = ["Wmt", "Wmf", "memxs"]

        def prefetch_chunk(ch):
            if ch < 0:
                return
            slot = ch % 4
            ov = PROMPT_ARENA_KEYS if ch in (13, 12) else []
            P.add("pool", dma(kbf[slot], kcT[ch].rearrange("p (b n) -> p b n", b=8)),
                  w=["kbf%d" % slot] + ov, dsem="d_kv%d" % slot)
            P.add("pool", dma(vbf[slot], vc[ch].rearrange("p (l b n) -> p l b n", l=2, b=8)),
                  w=["vbf%d" % slot] + ov, dsem="d_kv%d" % slot)
            P.group_last(["kbf%d" % slot, "vbf%d" % slot])

        def load_x(ti):
            t0 = ti * TW
            for hh in range(2):
                P.add("sp", dma(xs[:, 4 * hh:4 * hh + 4, :],
                                xT.rearrange("(c p) t -> p c t", p=128)[:, 4 * hh:4 * hh + 4, t0:t0 + TW]),
                      w=["xs%d" % hh], dsem="d_x%d" % hh)

        def tile_front(ti, nxt):
            sample = (ti == NT)
            t0 = ti * TW
            par = ti % 2
            qT = qT2[:, par, :]
            sg1 = sg1_2[:, par, :]
            qTk, sg1k = "qT%d" % par, "sg1_%d" % par
            psub, nsub = (64, 8) if sample else (128, 4)
            nb, L = (8, 64) if sample else (1, 512)
            Wd = 16 + L
            if sample:
                P.add("pool", dma(mkc_bf[64:128, :, :], mkcT), w=["mkc_bf"] + SETUP_ARENA_KEYS, dsem="d_h2")
                P.add("pool", dma(mvc_bf, mvc.rearrange("(m p) b f -> p m b f", p=128)), w=["mvc_bf"] + SETUP_ARENA_KEYS,
                      dsem="d_h2")
                P.group_last(["mkc_bf", "mvc_bf"])
                if "nocache" not in DEBUG_S:
                    prefetch_chunk(15)
                    prefetch_chunk(14)
            rms_stats(xs, ["xs0", "xs1"], TW, psub, nsub)
            if nxt is not None:
                load_x(nxt)

            def fm_proj(col0):
                b = next_pb()
                for c in range(8):
                    P.add("pe", mm(PB[b][:, :], Wg[:, c, col0:col0 + 128], xb[:, c, :], c == 0, c == 7),
                          r=[WG[c], XB[c]], rw=["PB%d" % b])
                return b

            def do_T0():
                b = fm_proj(0)
                P.add("dve", stt(qT, PB[b][:, :], 0.125, rstd_bc[:], ALU.mult, ALU.mult),
                      r=["PB%d" % b, "rstd_bc"], w=[qTk])

            def do_T1():
                b = fm_proj(128)
                kdst = kTs_bf if sample else kT_all[:, t0:t0 + TW]
                P.add("dve", tt(kdst, PB[b][:, :], rstd_bc[:], ALU.mult),
                      r=["PB%d" % b, "rstd_bc"], w=["kT%d" % ti] + (SETUP_ARENA_KEYS if sample else []))

            def do_T3():
                b = fm_proj(384)
                P.add("dve", tt(sg1, PB[b][:, :], rstd_bc[:], ALU.mult), r=["PB%d" % b, "rstd_bc"], w=[sg1k])

            def do_T4():
                b = fm_proj(512)
                if sample:
                    P.add("dve", mset(ubuf[0:64, :], 0.0), w=["ubuf"])
                    ub3 = ubuf[0:64, :].rearrange("p (b k) -> p b k", b=8)
                    P.add("sp", dma(ub3[:, :, 1:16], hist), rw=["ubuf"], dsem="d_h")
                    P.add("dve", tt(ub3[:, :, 16:80], PB[b][0:64, :].rearrange("p (b k) -> p b k", b=8),
                                    rstd_bc[0:64, :].rearrange("p (b k) -> p b k", b=8), ALU.mult),
                          r=["PB%d" % b, "rstd_bc"], rw=["ubuf"])
                else:
                    if ti == 0:
                        P.add("dve", mset(ubuf[0:64, 0:16], 0.0), rw=["ubuf"])
                    P.add("dve", tt(ubuf[0:64, 16:528], PB[b][0:64, :], rstd_bc[0:64, :], ALU.mult),
                          r=["PB%d" % b, "rstd_bc"], rw=["ubuf"])
                P.add("dve", stt(qx[64:128, :], PB[b][64:128, :], 0.125, rstd_bc[64:128, :], ALU.mult, ALU.mult),
                      r=["PB%d" % b, "rstd_bc"], w=["qx"])

            def do_T5():
                b = fm_proj(640)
                P.add("dve", tt(sg2[:], PB[b][:, :], rstd_bc[:], ALU.mult), r=["PB%d" % b, "rstd_bc"], w=["sg2"])


            def do_TM():
                for s in range(nsub):
                    if s % 2 == 0:
                        b = next_pb()
                    co = (s % 2) * 256
                    for c in range(8):
                        P.add("pe", mm(PB[b][0:psub, co:co + 256], xb[:, c, s * psub:(s + 1) * psub], Wg[:, c, 128:384],
                                       c == 0, c == 7),
                              r=[WG[c], XB[c]], rw=["PB%d" % b])
                    P.add("dve", tsm(kv32[0:psub, s, :], PB[b][0:psub, co:co + 256], rstd_tm[0:psub, s:s + 1]),
                          r=["PB%d" % b, "rstd_tm"], rw=["kv32"])
                if sample:
                    P.add("dve", cp(Vs_bf[0:64, :, :], kv32[0:64, :, 128:256]), r=["kv32"], w=["Vs32"] + SETUP_ARENA_KEYS)
                    P.add("pool", dma(k_out[t0:t0 + TW, :].rearrange("(s p) f -> p s f", p=64), kv32[0:64, :, 0:128]),
                          r=["kv32"], dsem="o_kv", final=True)
                    P.add("pool", dma(v_out[t0:t0 + TW, :].rearrange("(s p) f -> p s f", p=64), kv32[0:64, :, 128:256]),
                          r=["kv32"], dsem="o_kv", final=True)
                else:
                    P.add("dve", cp(V_all[:, 4 * ti:4 * ti + 4, :], kv32[:, 0:4, 128:256]),
                          r=["kv32"], w=["V%d" % ti])
                    P.add("pool", dma(k_out[t0:t0 + TW, :].rearrange("(s p) f -> p s f", p=128), kv32[:, 0:4, 0:128]),
                          r=["kv32"], dsem="o_kv", final=True)
                    P.add("pool", dma(v_out[t0:t0 + TW, :].rearrange("(s p) f -> p s f", p=128), kv32[:, 0:4, 128:256]),
                          r=["kv32"], dsem="o_kv", final=True)


            def silu_gate(g, tmp, key, tkey):
                P.add("act", actf(tmp[:], g, AF.Exp, scale=-1.0), r=[key], w=[tkey])
                P.add("act", actf(tmp[:], tmp[:], AF.Ln, bias=1.0), rw=[tkey])
                P.add("act", actf(tmp[:], tmp[:], AF.Exp, scale=-1.0), rw=[tkey])
                P.add("dve", tt(g, g, tmp[:], ALU.mult), r=[tkey], rw=[key])

            if sample:
                U = ubuf[0:64, :].rearrange("p (b k) -> p b k", b=8)
                A = sa[0:64, :].rearrange("p (b k) -> p b k", b=8)
                B_ = sbb[0:64, :].rearrange("p (b k) -> p b k", b=8)
                PA = pacc[0:64, :].rearrange("p (b k) -> p b k", b=8)
                PO = pooled[0:64, :].rearrange("p (b k) -> p b k", b=8)
                sl = lambda X, a, bb_: X[:, :, a:bb_]
            else:
                U = ubuf[0:64, 0:528]
                A = sa[0:64, 0:528]
                B_ = sbb[0:64, 0:528]
                PA = pacc[0:64, :]
                PO = pooled[0:64, :]
                sl = lambda X, a, bb_: X[:, a:bb_]
            full = lambda X: X[:, :, :] if sample else X[:, :]
            def pool_part0():
                P.add("dve", tt(sl(A, 1, Wd), sl(U, 1, Wd), sl(U, 0, Wd - 1), ALU.add), r=["ubuf"], w=["sa"])
                P.add("dve", tsm(full(PA), sl(A, 16, Wd), selw_t[0:64, 0:1]), r=["sa", "selw"], w=["pacc"])

            def pool_part1():
                P.add("dve", tt(sl(B_, 3, Wd), sl(A, 3, Wd), sl(A, 1, Wd - 2), ALU.add), r=["sa"], w=["sbb"])
                P.add("dve", stt(full(PA), sl(B_, 16, Wd), selw_t[0:64, 1:2], full(PA), ALU.mult, ALU.add),
                      r=["sbb", "selw"], rw=["pacc"])

            def pool_part2():
                P.add("dve", tt(sl(A, 7, Wd), sl(B_, 7, Wd), sl(B_, 3, Wd - 4), ALU.add), r=["sbb"], w=["sa"])
                P.add("dve", stt(full(PA), sl(A, 16, Wd), selw_t[0:64, 2:3], full(PA), ALU.mult, ALU.add),
                      r=["sa", "selw"], rw=["pacc"])

            def pool_part3():
                P.add("dve", tt(sl(B_, 15, Wd), sl(A, 15, Wd), sl(A, 7, Wd - 8), ALU.add), r=["sa"], w=["sbb"])
                P.add("dve", stt(full(PA), sl(B_, 16, Wd), selw_t[0:64, 3:4], full(PA), ALU.mult, ALU.add),
                      r=["sbb", "selw"], rw=["pacc"])
                if ti == 0:
                    P.add("dve", tt(pacc[0:64, :], pacc[0:64, :], corr_t[0:64, :], ALU.mult), r=["corr"], rw=["pacc"])
                P.add("dve", tt(full(PO), full(PA), sl(U, 16, Wd), ALU.subtract), r=["pacc", "ubuf"], w=["pooled"])
                if sample:
                    P.add("pool", dma(u_out[:, 15:135].rearrange("p (b k) -> p b k", b=8), U[:, :, 65:80]),
                          r=["ubuf"], dsem="o_misc", final=True)
                else:
                    if ti == NT - 1:
                        P.add("pool", dma(u_out[:, 0:15], ubuf[0:64, 513:528]), r=["ubuf"], dsem="o_misc", final=True)
                    P.add("dve", cp(ubuf[0:64, 0:16], ubuf[0:64, 512:528]), rw=["ubuf"])

            def pool_mm():
                b = next_pb()
                P.add("pe", mm(PB[b][0:64, :], poolw_bf[0:64, :], pooled[0:64, :], True, True),
                      r=["poolw_bf", "pooled"], rw=["PB%d" % b])
                P.add("dve", tsm(O2[0:64, :], PB[b][0:64, :], pscale_t[0:64, 0:1]), r=["PB%d" % b, "pscale"], rw=["O2"])


            def do_XA():
                bs = [next_pb(), next_pb()]
                for m in range(2):
                    if sample:
                        for bb in range(8):
                            P.add("pe", mm(PB[bs[m]][:, bb * 64:(bb + 1) * 64], mkc_bf[64:128, bb, m * 128:(m + 1) * 128],
                                           qx[64:128, bb * 64:(bb + 1) * 64], bb == 0, bb == 7),
                                  r=["mkc_bf", "qx"], rw=["PB%d" % bs[m]])
                    else:
                        P.add("pe", mm(PB[bs[m]][:, :], mkT_bf[64:128, m * 128:(m + 1) * 128], qx[64:128, :], True, True),
                              r=["mkT_bf", "qx"], rw=["PB%d" % bs[m]])
                    P.add("act", actf(pT[:, m, :], PB[bs[m]][:, :], AF.Exp), r=["PB%d" % bs[m]], w=["pT%d" % m])
                bo, bd = bs
                for m in range(2):
                    if sample:
                        for bb in range(8):
                            P.add("pe", mm(PB[bo][:, bb * 64:(bb + 1) * 64], mvc_bf[:, m, bb, :],
                                           pT[:, m, bb * 64:(bb + 1) * 64], (m == 0 and bb == 0), (m == 1 and bb == 7)),
                                  r=["mvc_bf", "pT%d" % m], rw=["PB%d" % bo])
                    else:
                        P.add("pe", mm(PB[bo][:, :], mvpad_bf[:, m, :], pT[:, m, :], m == 0, m == 1),
                              r=["mvpad", "pT%d" % m], rw=["PB%d" % bo])
                for m in range(2):
                    P.add("pe", mm(PB[bd][:, :], onespad_bf[:], pT[:, m, :], m == 0, m == 1),
                          r=["const4", "pT%d" % m], rw=["PB%d" % bd])
                P.add("act", actf(rden[64:128, :], PB[bd][64:128, :], AF.Ln), r=["PB%d" % bd], w=["rden"])
                P.add("act", actf(rden[64:128, :], rden[64:128, :], AF.Exp, scale=-1.0), rw=["rden"])
                P.add("dve", tt(O2[64:128, :], PB[bo][64:128, :], rden[64:128, :], ALU.mult),
                      r=["PB%d" % bo, "rden"], rw=["O2"])


            do_T4()
            do_T5()
            pool_part0()
            do_T3()
            silu_gate(sg2[:], tmpb, "sg2", "tmpb")
            pool_part1()
            do_T0()
            pool_part2()
            silu_gate(sg1, tmpa, sg1k, "tmpa")
            do_T1()
            pool_part3()
            do_TM()
            pool_mm()
            do_XA()
            P.add("dve", tt(Mt[:, par, 1, :], O2[:], sg2[:], ALU.mult), r=["O2", "sg2"], rw=["Mt%d" % par])

        def tile_attn(ti, bg):
            sample = (ti == NT)
            par = ti % 2
            qT = qT2[:, par, :]
            sg1 = sg1_2[:, par, :]
            qTk, sg1k = "qT%d" % par, "sg1_%d" % par
            blocks = []
            if sample:
                if "nonew" not in DEBUG_S:
                    blocks.append(dict(kind="new", c0=0, kp=64))
                if "nocache" not in DEBUG_S:
                    for kb in range(31, -1, -1):
                        blocks.append(dict(kind="cache", kb=kb, c0=0, kp=128))
            else:
                for kb in range(4 * ti + 3, -1, -1):
                    c0 = 128 * (kb - 4 * ti) if kb >= 4 * ti else 0
                    blocks.append(dict(kind="prompt", kb=kb, c0=c0, kp=128, diag=(kb >= 4 * ti)))
            NB = len(blocks)
            for i, bl in enumerate(blocks):
                bl["first"] = (i == 0)
                bl["last"] = (i == NB - 1)
            P.add("pool", mset(Sbf[:], 0.0), w=["Sbf0", "Sbf1"])
            gbase = state["blk"]
            state["blk"] += NB

            def zi(n):
                return (gbase + n) % 2

            def zkeys(n):
                return ["ZZ%d_0" % zi(n), "ZZ%d_1" % zi(n)]

            def ZZv(n):
                return ZZ[zi(n)].rearrange("p (h f) -> p h f", h=2)

            def sl2(n):
                return (gbase + n) % 2

            Ev = Ebuf.rearrange("p s (h f) -> p s h f", h=2)
            Lv = Lp.rearrange("p s (h f) -> p s h f", h=2)
            Wv = Wbuf.rearrange("p s (h f) -> p s h f", h=2)
            Sv = Sbf.rearrange("p s (h f) -> p s h f", h=2)

            if sample and "nocache" not in DEBUG_S:
                for ch0 in (13, 12):
                    prefetch_chunk(ch0)

            def addZ(n):
                bl = blocks[n]
                c0, kp = bl["c0"], bl["kp"]
                zz = ZZ[zi(n)]
                for h in range(2):
                    zk = ["ZZ%d_%d" % (zi(n), h)]
                    o = h * 512
                    if bl["kind"] == "prompt":
                        kb = bl["kb"]
                        P.add("pe", mm(zz[:, o + c0:o + 512], kT_all[64 * h:64 * h + 64, kb * 128:(kb + 1) * 128],
                                       qT[64 * h:64 * h + 64, c0:512], True, True),
                              r=["kT%d" % (kb // 4), qTk], w=zk)
                        if bl["diag"]:
                            P.add("pe", mm(zz[:, o + c0:o + c0 + 128], ident_bf[:], cmask_bf[:], False, True),
                                  r=["const2", "cmask_bf"], rw=zk)
                    elif bl["kind"] == "new":
                        for bb in range(8):
                            P.add("pe", mm(zz[0:64, o + bb * 64:o + (bb + 1) * 64],
                                           kTs_bf[64 * h:64 * h + 64, bb * 64:(bb + 1) * 64],
                                           qT[64 * h:64 * h + 64, bb * 64:(bb + 1) * 64], bb == 0, True),
                                  r=["kT%d" % NT, qTk], rw=zk)
                        P.add("pe", mm(zz[0:64, o:o + 512], ident_bf[0:64, 0:64], cmask_s_bf[0:64, :], False, True),
                              r=["const2", "cmask_s_bf"], rw=zk)
                    else:
                        kb = bl["kb"]
                        slot, loc = (kb // 2) % 4, kb % 2
                        for bb in range(8):
                            P.add("pe", mm(zz[:, o + bb * 64:o + (bb + 1) * 64],
                                           kbf[slot][64 * h:64 * h + 64, bb, loc * 128:(loc + 1) * 128],
                                           qT[64 * h:64 * h + 64, bb * 64:(bb + 1) * 64], bb == 0, True),
                                  r=["kbf%d" % slot, qTk], rw=zk)

            def addP1(n):
                bl = blocks[n]
                c0, kp = bl["c0"], bl["kp"]
                P.add("act", actf(Ev[0:kp, sl2(n), :, c0:512], ZZv(n)[0:kp, :, c0:512], AF.Exp),
                      r=zkeys(n), w=["E%d" % sl2(n)])

            def addP2(n):
                bl = blocks[n]
                c0, kp = bl["c0"], bl["kp"]
                P.add("act", actf(Lv[0:kp, sl2(n), :, c0:512], Ev[0:kp, sl2(n), :, c0:512], AF.Ln, bias=1.0),
                      r=["E%d" % sl2(n)], w=["Lp%d" % sl2(n)])

            def addSA(n):
                bl = blocks[n]
                if bl["last"]:
                    return
                c0, kp = bl["c0"], bl["kp"]
                cur, nx = sl2(n), sl2(n + 1)
                P.add("dve", tt(Sv[0:kp, nx, :, c0:512], Sv[0:kp, cur, :, c0:512], Lv[0:kp, sl2(n), :, c0:512], ALU.add),
                      r=["Lp%d" % sl2(n), "Sbf%d" % cur], rw=["Sbf%d" % nx])

            def addNO(n):
                bl = blocks[n]
                if bl["first"]:
                    return
                c0, kp = bl["c0"], bl["kp"]
                zz = ZZ[zi(n)]
                ss = sl2(n)
                for h in range(2):
                    o = h * 512
                    P.add("pe", mm(zz[0:kp, o + c0:o + 512], negones_bf[:, 0:kp], Sv[:, ss, h, c0:512], False, False),
                          r=["const1", "Sbf%d" % ss], rw=["ZZ%d_%d" % (zi(n), h)])

            def addTRI(n):
                bl = blocks[n]
                c0, kp = bl["c0"], bl["kp"]
                zz = ZZ[zi(n)]
                for h in range(2):
                    o = h * 512
                    P.add("pe", mm(zz[0:kp, o + c0:o + 512], tri_bf[0:kp, 0:kp], Lv[0:kp, sl2(n), h, c0:512], False, True),
                          r=["const0", "Lp%d" % sl2(n)], rw=["ZZ%d_%d" % (zi(n), h)])

            def addP3(n):
                bl = blocks[n]
                c0, kp = bl["c0"], bl["kp"]
                P.add("act", actf(Wv[0:kp, sl2(n), :, c0:512], ZZv(n)[0:kp, :, c0:512], AF.Exp),
                      r=zkeys(n), w=["W%d" % sl2(n)])

            def addAV(n):
                bl = blocks[n]
                c0, kp = bl["c0"], bl["kp"]
                for h in range(2):
                    acc = ACC[h]
                    ak = "ACC%d" % h
                    if bl["kind"] == "prompt":
                        kb = bl["kb"]
                        P.add("pe", mm(acc[:, c0:512], V_all[:, kb, :], Wv[:, sl2(n), h, c0:512], bl["first"], bl["last"]),
                              r=["V%d" % (kb // 4), "W%d" % sl2(n)], rw=[ak])
                    elif bl["kind"] == "new":
                        for bb in range(8):
                            P.add("pe", mm(acc[:, bb * 64:(bb + 1) * 64], Vs_bf[0:64, bb, :],
                                           Wv[0:64, sl2(n), h, bb * 64:(bb + 1) * 64], bb == 0, False),
                                  r=["Vs32", "W%d" % sl2(n)], rw=[ak])
                    else:
                        kb = bl["kb"]
                        slot, loc = (kb // 2) % 4, kb % 2
                        for bb in range(8):
                            P.add("pe", mm(acc[:, bb * 64:(bb + 1) * 64], vbf[slot][:, loc, bb, :],
                                           Wv[:, sl2(n), h, bb * 64:(bb + 1) * 64], False, bl["last"]),
                                  r=["vbf%d" % slot, "W%d" % sl2(n)], rw=[ak])
                if bl["kind"] == "cache" and bl["kb"] % 2 == 0:
                    prefetch_chunk(bl["kb"] // 2 - 4)

            nodes = P.plan_bg(bg)
            cnt = {}
            for nd in nodes:
                cnt[nd["eng"]] = cnt.get(nd["eng"], 0) + 1
            base = {"pe": 5, "dve": 2, "act": 1, "pool": 2, "sp": 2}
            nit = max(NB - 2, 1)
            caps = {e: max(base[e], -(-cnt.get(e, 0) // nit)) for e in base}
            if NB > 0:
                addZ(0)
            for n in range(0, NB + 1):
                if n < NB:
                    addP1(n)
                    addNO(n)
                if n - 1 >= 0:
                    addP3(n - 1)
                if n + 1 < NB:
                    addZ(n + 1)
                if n < NB:
                    addP2(n)
                    addSA(n)
                    addTRI(n)
                if n - 1 >= 0:
                    addAV(n - 1)
                if n < NB:
                    P.issue_bg(nodes, n, caps)
                if sample and n == 14:
                    emit_prompt_gathers()
            P.issue_bg(nodes, NB + 1, caps, flush=True)

            slot = par
            P.add("dve", tt(Mt[0:64, slot, 0, :], ACC[0][0:64, :], sg1[0:64, :], ALU.mult),
                  r=["ACC0", sg1k], rw=["Mt%d" % slot])
            P.add("dve", tt(Mt[64:128, slot, 0, :], ACC[1][64:128, :], sg1[64:128, :], ALU.mult),
                  r=["ACC1", sg1k], rw=["Mt%d" % slot])
            if DEBUG_DUMP:
                P.add("dve", cp(tmpa[0:64, :], ACC[0][0:64, :]), r=["ACC0"], rw=["tmpa"])
                P.add("dve", cp(tmpa[64:128, :], ACC[1][64:128, :]), r=["ACC1"], rw=["tmpa"])
                P.add("pool", dma(dbg[ti, 0], tmpa[:]), r=["tmpa"], dsem="o_dbg", final=True)
                P.add("pool", dma(dbg[ti, 1], O2[:]), r=["O2"], dsem="o_dbg", final=True)
                P.add("pool", dma(dbg[ti, 2], sg1), r=[sg1k], dsem="o_dbg", final=True)
                P.add("pool", dma(dbg[ti, 3], sg2[:]), r=["sg2"], dsem="o_dbg", final=True)
                P.add("pool", dma(dbgM[ti].rearrange("t p f -> p t f"), Mt[:, slot, :, :]), r=["Mt%d" % slot], dsem="o_dbg", final=True)
            if sample:
                ccs = cc_in_s.rearrange("(j t p) f -> j t p f", j=4, t=2)
                for t in range(2):
                    P.add("pool", dma(ccs[:, t, :, :].rearrange("j p f -> p j f"),
                                      Mt[:, slot, t, :].rearrange("p (j f) -> p j f", j=4)),
                          r=["Mt%d" % slot], rw=["cc_in_s"], dsem="o_m%d" % slot)
                P.add("pool", lambda e: e.collective_compute("AllGather", ALU.bypass, replica_groups=RG,
                                                             ins=[cc_in_s], outs=[cc_out_s]),
                      r=["cc_in_s"], w=["cc_out_s"], dsem="d_cc", inc=1)
            else:
                g, cc0 = ti // 4, (ti % 4) * 512
                ccv = cc_in.rearrange("(g t p) f -> g t p f", g=4, t=2)
                P.add("pool", dma(ccv[g, :, :, cc0:cc0 + 512].rearrange("t p f -> p t f"), Mt[:, slot, :, :]),
                      r=["Mt%d" % slot], rw=["cc_in%d" % g], dsem="o_m%d" % slot)
                if ti % 4 == 3 and not DEBUG_NO_P4:
                    P.add("pool", (lambda g_: (lambda e: e.collective_compute(
                        "AllGather", ALU.bypass, replica_groups=RG,
                        ins=[cc_in[g_ * 256:(g_ + 1) * 256, :]], outs=[cc_out[g_ * 1024:(g_ + 1) * 1024, :]])))(g),
                          r=["cc_in%d" % g], w=["cc_out%d" % g], dsem="d_cc", inc=1)

        gath = {}

        def gather_dma(r, smp):
            def fn(e):
                if "my" not in gath:
                    gath["my"] = e.partition_id() % 4
                if smp:
                    view = cc_out_s.rearrange("(r j t p) f -> j p r t f", r=4, j=4, t=2)
                    src = view[bass.ds(gath["my"], 1)][0]
                    return e.dma_start(out=MGs[:, 2 * r:2 * r + 2, :], in_=src[:, r, :, :])
                view = cc_out.rearrange("(g r t p) f -> g p r t f", g=4, r=4, t=2)
                src = view[bass.ds(gath["my"], 1)][0]
                return e.dma_start(out=MG[:, 2 * r:2 * r + 2, 0:2048], in_=src[:, r, :, :])
            return fn

        gstate = {"done": False}

        def emit_prompt_gathers():
            if gstate["done"] or DEBUG_NO_P4:
                return
            gstate["done"] = True
            mg_reads = ["kT%d" % i for i in range(NT)] + ["xs0", "xs1"]
            cco = ["cc_out%d" % g for g in range(4)]
            tl = None
            for ch in range(4):
                tl = P.add("pool", gather_dma(ch, False), r=cco, w=["MGc%d" % ch] + (mg_reads + ["MG0", "MG1"] if ch == 0 else []),
                           dsem="d_g")
            P.group(["MG0", "MG1"], tl)

        P4_SCRATCH = ["kbf0", "kbf1", "vbf0", "vbf1"] + ["V%d" % i for i in range(NT)]

        ntiles = NT if DEBUG_TILES is None else DEBUG_TILES
        tiles = list(range(ntiles)) + ([] if DEBUG_NO_SAMPLE else [NT])
        if tiles:
            load_x(tiles[0])
            tile_front(tiles[0], tiles[1] if len(tiles) > 1 else None)
        for i, ti in enumerate(tiles):
            bg = []
            if i + 1 < len(tiles):
                P.begin_defer()
                tile_front(tiles[i + 1], tiles[i + 2] if i + 2 < len(tiles) else None)
                bg = P.end_defer()
            tile_attn(ti, bg)

        if not DEBUG_NO_P4:
            emit_prompt_gathers()

            def sample_gathers():
                tl = None
                for ch in range(4):
                    tl = P.add("pool", gather_dma(ch, True), r=["cc_out_s"], w=["MGsc%d" % ch] + (["MGs"] if ch == 0 else []),
                               dsem="d_gs")
                P.group(["MGs"], tl)
            if DEBUG_DUMP:
                P.add("pool", dma(dbgMG, MG), r=["MG0", "MG1"], dsem="o_dbg", final=True)
            P.add("sp", dma(gfin_t, gfin), w=XB[0:4], dsem="d_h")

            if not DEBUG_NO_SAMPLE:
                sample_gathers()
            nst = OUTTOK // 128
            for st in range(nst):
                slot = st % 2
                smp_st = (st == nst - 1)
                P.add("sp", dma(xr[slot], xres[st * 128:(st + 1) * 128, :]),
                      w=["xr%d" % slot] + (["kv32"] if st < 2 else []), dsem="d_x%d" % slot)
                y1 = y1s[slot]
                y1k = "y1_%d" % slot
                for nh in range(2):
                    zb = ZZ[st % 2][:, nh * 512:(nh + 1) * 512]
                    zk = "ZZ%d_%d" % (st % 2, nh)
                    for ch in range(8):
                        lhs = MGs[:, ch, :] if smp_st else MG[:, ch, st * 128:(st + 1) * 128]
                        P.add("pe", mm(zb, lhs, WO[:, ch, nh * 512:(nh + 1) * 512], ch == 0, ch == 7),
                              r=[("MGs" if smp_st else "MG%d" % (ch // 4)), "WO"], rw=[zk])
                    P.add("dve", tt(y1[:, nh * 512:(nh + 1) * 512], zb, xr[slot][:, nh * 512:(nh + 1) * 512], ALU.add),
                          r=[zk, "xr%d" % slot], rw=[y1k], w=(P4_SCRATCH if st < 2 else []))
                if DEBUG_DUMP:
                    P.add("pool", dma(dbgY1[st * 128:(st + 1) * 128, :], y1), r=[y1k], dsem="o_dbg", final=True)
                P.add("act", actf(yo[slot], y1, AF.Square), r=[y1k], w=["yo%d" % slot] + (P4_SCRATCH if st < 2 else []))
                P.add("dve", lambda e, sl_=slot: e.reduce_sum(out=ssq_t[:, 0:1], in_=yo[sl_], axis=mybir.AxisListType.X),
                      r=["yo%d" % slot], w=["ssq"])
                P.add("act", actf(ssq_t[:, 1:2], ssq_t[:, 0:1], AF.Ln, scale=1.0 / D, bias=EPS), r=["ssq"], w=["ssq1"])
                P.add("act", actf(ssq_t[:, 2:3], ssq_t[:, 1:2], AF.Exp, scale=-0.5), r=["ssq1"], w=["ssq2"])
                P.add("dve", stt(yo[slot], y1, ssq_t[:, 2:3], gfin_t, ALU.mult, ALU.mult),
                      r=[y1k, "ssq2"] + XB[0:4], rw=["yo%d" % slot])
                P.add("act", dma(y_out[st * 128:(st + 1) * 128, :], yo[slot]), r=["yo%d" % slot],
                      dsem="o_y%d" % slot, final=True)


        semnames = P.finalize()
        sems = {n: es.enter_context(nc.semaphore(n)) for n in semnames}
        with nc.Block() as block:
            @block.sync
            def _(e):
                P.emit("sp", e, sems)

            @block.tensor
            def _(e):
                P.emit("pe", e, sems)

            @block.scalar
            def _(e):
                P.emit("act", e, sems)

            @block.vector
            def _(e):
                P.emit("dve", e, sems)

            @block.gpsimd
            def _(e):
                P.emit("pool", e, sems, final_waits=True)
    return nc


_NC_CACHE = {}


def _consts():
    k = np.arange(128)[:, None]
    m = np.arange(128)[None, :]
    tri = np.where(k >= m, -1.0, 0.0).astype(np.float32)
    negones = -np.ones((128, 128), np.float32)
    ident = np.eye(128, dtype=np.float32)
    ones = np.ones((128, 128), np.float32)
    onespad = np.concatenate([np.zeros((128, 64), np.float32), np.ones((128, 64), np.float32)], axis=1)
    cst = np.stack([tri, negones, ident, ones, onespad], axis=1)
    cmask = np.where(k >= m, NEG, 0.0).astype(np.float32)
    k64 = np.arange(128)[:, None]
    f = np.arange(512)[None, :] % 64
    cmask_s = np.where(k64 >= f, NEG, 0.0).astype(np.float32)
    return np.ascontiguousarray(cst), cmask, np.ascontiguousarray(cmask_s)


def _prep_inputs(c, x_prompt, x_sample, cache_sb_k, cache_sb_v, state_pool, cache_mem_k, cache_mem_v, mem_prompt,
                 g_norm, w_in, pool_w, pool_scale, g_mem, w_mem_kv, w_out, g_final):
    b, j = c // 4, c % 4
    sb_ = slice(8 * b, 8 * b + 8)
    f32 = lambda a: np.ascontiguousarray(a, dtype=np.float32)
    xs_s = x_sample[sb_].reshape(512, D)
    xT = np.concatenate([x_prompt[b].T, xs_s.T], axis=1)
    cols = np.concatenate([
        np.arange(128 * j, 128 * j + 128),
        512 + np.arange(128 * j, 128 * j + 128),
        1024 + np.arange(128 * j, 128 * j + 128),
        1536 + np.arange(128 * j, 128 * j + 128),
        2048 + np.arange(64 * j, 64 * j + 64),
        2560 + np.arange(64 * j, 64 * j + 64),
        2304 + np.arange(64 * j, 64 * j + 64),
        2816 + np.arange(64 * j, 64 * j + 64)])
    wsel = w_in[0][:, cols]
    wmsel = w_mem_kv[0][:, np.concatenate([np.arange(64 * j, 64 * j + 64), 256 + np.arange(64 * j, 64 * j + 64)])]
    wins = np.array([2, 4, 8, 16], np.float32)
    selw = np.zeros((64, 4), np.float32)
    selw[:, j] = 1.0 / wins[j]
    t = np.arange(512, dtype=np.float32)
    corr = np.broadcast_to(wins[j] / np.minimum(t + 1.0, wins[j]), (64, 512))
    rows = []
    for r in range(4):
        rows.append(np.arange(128 * r, 128 * r + 128))
        rows.append(np.concatenate([512 + np.arange(64 * r, 64 * r + 64), 768 + np.arange(64 * r, 64 * r + 64)]))
    wout = w_out[0][np.concatenate(rows), :]
    xres = np.concatenate([x_prompt[b, 2048 * j:2048 * j + 2048],
                           x_sample[8 * b + 2 * j:8 * b + 2 * j + 2].reshape(128, D)], axis=0)
    kc = cache_sb_k[0, sb_, :, 2 * j:2 * j + 2, :]
    kcT = kc.reshape(8, 16, 256, 2, 64).transpose(1, 3, 4, 0, 2).reshape(16, 128, 8 * 256)
    vcs = cache_sb_v[0, sb_, :, 2 * j:2 * j + 2, :]
    vcl = vcs.reshape(8, 16, 2, 128, 128).transpose(1, 3, 2, 0, 4).reshape(16, 128, 2 * 8 * 128)
    histl = state_pool[0, sb_, :, 64 * j:64 * j + 64].transpose(2, 0, 1)
    mkcT = cache_mem_k[0, sb_, :, j, :].transpose(2, 0, 1)
    mvc = np.zeros((256, 8, 128), np.float32)
    mvc[:, :, 64:128] = cache_mem_v[0, sb_, :, j, :].transpose(1, 0, 2)
    cst, cmask, cmask_s = _consts()
    return {
        "xT": f32(xT), "wsel": f32(wsel), "wmsel": f32(wmsel), "memT": f32(mem_prompt[b].T),
        "gn": f32(g_norm[0].reshape(8, 128).T), "gm": f32(g_mem[0].reshape(8, 128).T),
        "poolw": f32(pool_w[0, j]), "pscale": f32(pool_scale[0, 64 * j:64 * j + 64].reshape(64, 1)),
        "selw": selw, "corr": f32(corr), "wout": f32(wout),
        "gfin": f32(np.broadcast_to(g_final[None, :], (128, D))), "xres": f32(xres),
        "kcT": f32(kcT), "vc": f32(vcl), "hist": f32(histl), "mkcT": f32(mkcT), "mvc": mvc,
        "cst": cst, "cmask": cmask, "cmask_s": cmask_s,
    }


def kernel(x_prompt, x_sample, cache_sb_k, cache_sb_v, state_pool, cache_mem_k, cache_mem_v, mem_prompt,
           g_norm, w_in, pool_w, pool_scale, g_mem, w_mem_kv, w_out, g_final):
    args = [np.asarray(a) for a in (x_prompt, x_sample, cache_sb_k, cache_sb_v, state_pool, cache_mem_k,
                                    cache_mem_v, mem_prompt, g_norm, w_in, pool_w, pool_scale, g_mem,
                                    w_mem_kv, w_out, g_final)]
    if "nc" not in _NC_CACHE:
        _NC_CACHE["nc"] = build_nc()
    nc = _NC_CACHE["nc"]
    in_maps = [_prep_inputs(c, *args) for c in range(NCORES)]
    res = run_bass_kernel_spmd(nc, in_maps, core_ids=list(range(NCORES)))
    R = res.results

    y_prompt = np.zeros((2, SEQ, D), np.float32)
    y_sample = np.zeros((16, 64, D), np.float32)
    sb_k_p = np.zeros((1, 2, SEQ, 8, 64), np.float32)
    sb_v_p = np.zeros((1, 2, SEQ, 8, 64), np.float32)
    pool_p = np.zeros((1, 2, 15, 256), np.float32)
    mem_k_p = np.zeros((1, 2, 256, 4, 64), np.float32)
    mem_v_p = np.zeros((1, 2, 256, 4, 64), np.float32)
    sb_k_s = np.zeros((1, 16, 64, 8, 64), np.float32)
    sb_v_s = np.zeros((1, 16, 64, 8, 64), np.float32)
    pool_s = np.zeros((1, 16, 15, 256), np.float32)
    for c in range(NCORES):
        b, j = c // 4, c % 4
        r = R[c]
        yo = np.asarray(r["y_out"])
        y_prompt[b, 2048 * j:2048 * j + 2048] = yo[0:2048]
        y_sample[8 * b + 2 * j:8 * b + 2 * j + 2] = yo[2048:2176].reshape(2, 64, D)
        ko, vo = np.asarray(r["k_out"]), np.asarray(r["v_out"])
        sb_k_p[0, b, :, 2 * j:2 * j + 2, :] = ko[0:SEQ].reshape(SEQ, 2, 64)
        sb_v_p[0, b, :, 2 * j:2 * j + 2, :] = vo[0:SEQ].reshape(SEQ, 2, 64)
        sb_k_s[0, 8 * b:8 * b + 8, :, 2 * j:2 * j + 2, :] = ko[SEQ:].reshape(8, 64, 2, 64)
        sb_v_s[0, 8 * b:8 * b + 8, :, 2 * j:2 * j + 2, :] = vo[SEQ:].reshape(8, 64, 2, 64)
        uo = np.asarray(r["u_out"])
        pool_p[0, b, :, 64 * j:64 * j + 64] = uo[:, 0:15].T
        pool_s[0, 8 * b:8 * b + 8, :, 64 * j:64 * j + 64] = uo[:, 15:135].reshape(64, 8, 15).transpose(1, 2, 0)
        mo = np.asarray(r["memkv_out"])
        mem_k_p[0, b, :, j, :] = mo[:, 0:64]
        mem_v_p[0, b, :, j, :] = mo[:, 64:128]
    return (y_prompt, y_sample, sb_k_p, sb_v_p, pool_p, mem_k_p, mem_v_p, sb_k_s, sb_v_s, pool_s)
```

```python
import contextlib
import numpy as np
import concourse.bass as bass
import concourse.mybir as mybir
from concourse.bass_utils import run_bass_kernel_spmd

F32 = mybir.dt.float32
BF16 = mybir.dt.bfloat16
AF = mybir.ActivationFunctionType
ALU = mybir.AluOpType

NCORES = 8
D = 1024
SEQ = 8192
NT = 16
TW = 512
PAST = 4096
EPS = 1e-6
NEG = -30000.0
NTOK = SEQ + 512
OUTTOK = 2048 + 128

import os
DEBUG_TILES = int(os.environ["KDBG_TILES"]) if "KDBG_TILES" in os.environ else None
DEBUG_NO_P4 = "KDBG_NOP4" in os.environ
DEBUG_NO_SAMPLE = "KDBG_NOSAMPLE" in os.environ
DEBUG_DUMP = "KDBG_DUMP" in os.environ
DEBUG_S = os.environ.get("KDBG_S", "")


class Tok:
    __slots__ = ("eng", "sem", "inc", "value", "used")

    def __init__(self, eng, sem, inc):
        self.eng, self.sem, self.inc, self.value, self.used = eng, sem, inc, None, False


class Prog:
    ENGS = ("pe", "act", "dve", "pool", "sp")

    def __init__(self):
        self.ops = {e: [] for e in self.ENGS}
        self.writer = {}
        self.readers = {}
        self.final = []
        self.defer = None

    def begin_defer(self):
        self.defer = []

    def end_defer(self):
        d, self.defer = self.defer, None
        return d

    def run(self, lst, k):
        n = 0
        while lst and n < k:
            a = lst.pop(0)
            if a[0] == "__group__":
                self.group(a[1], self._last_tok)
            else:
                self._last_tok = self.add(*a[0], **a[1])
            n += 1

    def plan_bg(self, lst):
        nodes = []
        writer, readers = {}, {}
        for it in lst:
            if it[0] == "__group__":
                nodes[-1]["grp"] = it[1]
                for k in it[1]:
                    writer[k] = len(nodes) - 1
                    readers[k] = set()
                continue
            (eng, fn), kw = it
            deps = set()
            for k in list(kw["r"]) + list(kw["rw"]):
                if k in writer:
                    deps.add(writer[k])
            for k in list(kw["w"]) + list(kw["rw"]):
                if k in writer:
                    deps.add(writer[k])
                deps |= readers.get(k, set())
            idx = len(nodes)
            for k in kw["r"]:
                readers.setdefault(k, set()).add(idx)
            for k in list(kw["w"]) + list(kw["rw"]):
                writer[k] = idx
                readers[k] = set()
            deps.discard(idx)
            nodes.append(dict(item=it, deps=deps, eng=eng, dma=kw["dsem"] is not None, grp=None, issued=None))
        return nodes

    def issue_bg(self, nodes, n, caps, flush=False):
        used = {}
        for nd in nodes:
            if nd["issued"] is not None:
                continue
            ok = True
            for d in nd["deps"]:
                dn = nodes[d]
                if dn["issued"] is None:
                    ok = False
                    break
                if dn["eng"] == nd["eng"] and not dn["dma"]:
                    lag = 0
                elif nd["eng"] in ("act", "pe"):
                    lag = 4 if dn["dma"] else (3 if dn["eng"] == "pool" else 1)
                else:
                    lag = 3 if dn["dma"] else 0
                if not flush and dn["issued"] + lag > n:
                    ok = False
                    break
            if not ok:
                continue
            e = nd["eng"]
            if not flush and used.get(e, 0) >= caps.get(e, 1):
                continue
            self.add(*nd["item"][0], **nd["item"][1])
            if nd["grp"]:
                self.group(nd["grp"], self._last_tok)
            nd["issued"] = n
            used[e] = used.get(e, 0) + 1

    def add(self, eng, fn, r=(), w=(), rw=(), dsem=None, extra=(), final=False, inc=None):
        if self.defer is not None:
            self.defer.append(((eng, fn), dict(r=r, w=w, rw=rw, dsem=dsem, extra=extra, final=final, inc=inc)))
            return None
        sem = dsem if dsem is not None else "p_" + eng
        tok = Tok(eng, sem, inc if inc is not None else (16 if dsem is not None else 1))
        deps = []
        for k in list(r) + list(rw):
            t = self.writer.get(k)
            if t is not None:
                deps.append(t)
        for k in list(w) + list(rw):
            t = self.writer.get(k)
            if t is not None:
                deps.append(t)
            deps.extend(self.readers.get(k, {}).values())
        deps.extend(t for t in extra if t is not None)
        for k in r:
            self.readers.setdefault(k, {})[sem] = tok
        for k in list(w) + list(rw):
            self.writer[k] = tok
            self.readers[k] = {}
        deps = [d for d in deps if not (d.eng == "pe" and eng == "pe")]
        for d in deps:
            d.used = True
        if dsem is not None:
            tok.used = True
        if final:
            tok.used = True
            self.final.append(tok)
        self.ops[eng].append((fn, deps, tok))
        self._last_tok = tok
        return tok

    def group_last(self, keys):
        if self.defer is not None:
            self.defer.append(("__group__", list(keys)))
        else:
            self.group(keys, self._last_tok)

    def group(self, keys, tok):
        for k in keys:
            self.writer[k] = tok
            self.readers[k] = {}

    def finalize(self):
        counts = {}
        for e in self.ENGS:
            for fn, deps, tok in self.ops[e]:
                if tok.used:
                    counts[tok.sem] = counts.get(tok.sem, 0) + tok.inc
                    tok.value = counts[tok.sem]
        return sorted(counts.keys())

    def emit(self, eng_name, e, sems, final_waits=False):
        waited = {}
        for fn, deps, tok in self.ops[eng_name]:
            need = {}
            for d in deps:
                if need.get(d.sem, 0) < d.value:
                    need[d.sem] = d.value
            for s, v in need.items():
                if waited.get(s, 0) < v:
                    e.wait_ge(sems[s], v)
                    waited[s] = v
            ins = fn(e)
            if tok.used:
                ins.then_inc(sems[tok.sem], tok.inc)
        if final_waits:
            need = {}
            for t in self.final:
                if need.get(t.sem, 0) < t.value:
                    need[t.sem] = t.value
            for s, v in need.items():
                e.wait_ge(sems[s], v)


def build_nc():
    nc = bass.Bass("TRN2", target_bir_lowering=False)
    P = Prog()

    def din(name, shape, dt=F32):
        return nc.dram_tensor(name, list(shape), dt, kind="ExternalInput").ap()

    def dout(name, shape, dt=F32):
        return nc.dram_tensor(name, list(shape), dt, kind="ExternalOutput").ap()

    xT = din("xT", [D, NTOK])
    wsel = din("wsel", [D, 768])
    wmsel = din("wmsel", [D, 128])
    memT = din("memT", [D, 256])
    gn = din("gn", [128, 8])
    gm = din("gm", [128, 8])
    poolw = din("poolw", [64, 64])
    pscale = din("pscale", [64, 1])
    selw = din("selw", [64, 4])
    corr = din("corr", [64, 512])
    wout = din("wout", [D, D])
    gfin = din("gfin", [128, D])
    xres = din("xres", [OUTTOK, D])
    kcT = din("kcT", [16, 128, 8 * 256])
    vc = din("vc", [16, 128, 2 * 8 * 128])
    hist = din("hist", [64, 8, 15])
    mkcT = din("mkcT", [64, 8, 256])
    mvc = din("mvc", [256, 8, 128])
    cst = din("cst", [128, 5, 128])
    cmask = din("cmask", [128, 128])
    cmask_s = din("cmask_s", [128, 512])

    k_out = dout("k_out", [NTOK, 128])
    v_out = dout("v_out", [NTOK, 128])
    u_out = dout("u_out", [64, 135])
    memkv_out = dout("memkv_out", [256, 128])
    y_out = dout("y_out", [OUTTOK, D])
    if DEBUG_DUMP:
        dbg = dout("dbg", [17, 4, 128, 512])
        dbgM = dout("dbgM", [17, 2, 128, 512], BF16)
        dbgMG = dout("dbgMG", [128, 8, 2176], BF16)
        dbgY1 = dout("dbgY1", [OUTTOK, D])

    cc_in = nc.dram_tensor("cc_in", [4 * 256, 2048], BF16, kind="Internal").ap()
    cc_out = nc.dram_tensor("cc_out", [4 * 4 * 256, 2048], BF16, kind="Internal").ap()
    cc_in_s = nc.dram_tensor("cc_in_s", [4 * 256, 128], BF16, kind="Internal").ap()
    cc_out_s = nc.dram_tensor("cc_out_s", [4 * 4 * 256, 128], BF16, kind="Internal").ap()
    RG = [[0, 1, 2, 3], [4, 5, 6, 7]]

    es = contextlib.ExitStack()
    with es:
        def sb(name, shape, dt):
            return es.enter_context(nc.sbuf_tensor(name, list(shape), dt))

        def ps(name):
            return es.enter_context(nc.psum_tensor(name, [128, 512], F32))

        ARENA_W = 16384
        arena = sb("arena", [128, ARENA_W], F32)

        def carve(off_bytes, nbytes, dt, pattern=None, **kw):
            a = arena[:, off_bytes // 4:(off_bytes + nbytes) // 4]
            if dt != F32:
                a = a.bitcast(dt)
            if pattern is not None:
                a = a.rearrange(pattern, **kw)
            return a

        Wg = sb("Wg", [128, 8, 768], BF16)
        WO = sb("WO", [128, 8, 1024], BF16)
        tri_bf = sb("tri_bf", [128, 128], BF16)
        negones_bf = sb("negones_bf", [128, 128], BF16)
        ident_bf = sb("ident_bf", [128, 128], BF16)
        ones_bf = sb("ones_bf", [128, 128], BF16)
        onespad_bf = sb("onespad_bf", [128, 128], BF16)
        cmask_bf = sb("cmask_bf", [128, 128], BF16)
        cmask_s_bf = sb("cmask_s_bf", [128, 512], BF16)
        one32 = sb("one32", [128, 1], F32)
        gn_t = sb("gn_t", [128, 8], F32)
        gm_t = sb("gm_t", [128, 8], F32)
        selw_t = sb("selw_t", [128, 4], F32)
        pscale_t = sb("pscale_t", [128, 1], F32)
        corr_t = sb("corr_t", [128, 512], F32)
        poolw32 = sb("poolw32", [128, 64], F32)
        poolw_bf = sb("poolw_bf", [128, 64], BF16)
        mkT_bf = sb("mkT_bf", [128, 256], BF16)
        mvpad_bf = sb("mvpad_bf", [128, 2, 128], BF16)
        xb = sb("xb", [128, 8, 512], BF16)
        xsq = sb("xsq", [128, 8, 512], BF16)
        lnms = sb("lnms", [128, 512], F32)
        rstd_bc = sb("rstd_bc", [128, 512], F32)
        rstd_tm = sb("rstd_tm", [128, 8], F32)
        qT2 = sb("qT2", [128, 2, 512], BF16)
        sg1_2 = sb("sg1_2", [128, 2, 512], F32)
        sg2 = sb("sg2", [128, 512], F32)
        tmpa = sb("tmpa", [128, 512], F32)
        tmpb = sb("tmpb", [128, 512], F32)
        qx = sb("qx", [128, 512], BF16)
        kv32 = sb("kv32", [128, 8, 256], F32)
        ubuf = sb("ubuf", [128, 640], F32)
        sa = sb("sa", [128, 640], F32)
        sbb = sb("sbb", [128, 640], F32)
        pacc = sb("pacc", [128, 512], F32)
        pooled = sb("pooled", [128, 512], BF16)
        pT = sb("pT", [128, 2, 512], BF16)
        rden = sb("rden", [128, 512], F32)
        O2 = sb("O2", [128, 512], F32)
        Ebuf = sb("Ebuf", [128, 2, 1024], F32)
        Lp = sb("Lp", [128, 2, 1024], BF16)
        Wbuf = sb("Wbuf", [128, 2, 1024], BF16)
        Sbf = sb("Sbf", [128, 2, 1024], BF16)
        Mt = sb("Mt", [128, 2, 2, 512], BF16)

        ZZ = [es.enter_context(nc.psum_tensor("ZZ%d" % i, [128, 1024], F32)) for i in range(2)]
        ACC = [ps("ACC%d" % i) for i in range(2)]
        PB = [ps("PB%d" % i) for i in range(2)]

        wst = carve(16384, 24576, F32, "p (c n) -> p c n", c=8)
        wmst = carve(40960, 4096, F32, "p (c n) -> p c n", c=8)
        cst_st = carve(45056, 2560, F32, "p (c n) -> p c n", c=5)
        cmask_st = carve(47616, 512, F32)
        cmask_s_st = carve(48128, 2048, F32)
        memxs = carve(50176, 8192, F32, "p (c n) -> p c n", c=8)
        Wmt = carve(58368, 2048, BF16, "p (c n) -> p c n", c=8)
        Wmf = carve(60416, 2048, BF16, "p (c n) -> p c n", c=8)
        xs = carve(0, 16384, F32, "p (c n) -> p c n", c=8)
        kT_all = carve(16384, 17408, BF16)
        V_all = carve(33792, 16384, BF16, "p (k n) -> p k n", k=64)
        kbf = [carve(34816 + 4096 * i, 4096, BF16, "p (b n) -> p b n", b=8) for i in range(2)]
        vbf = [carve(43008 + 4096 * i, 4096, BF16, "p (l b n) -> p l b n", l=2, b=8) for i in range(2)]
        cslots = sb("cslots", [128, 4, 2048], BF16)
        kbf += [cslots[:, i, :].rearrange("p (b n) -> p b n", b=8) for i in range(2)]
        vbf += [cslots[:, 2 + i, :].rearrange("p (l b n) -> p l b n", l=2, b=8) for i in range(2)]
        kTs_bf = carve(61440, 1024, BF16)
        Vs_bf = carve(51200, 2048, BF16, "p (b n) -> p b n", b=8)
        mkc_bf = carve(53248, 4096, BF16, "p (b n) -> p b n", b=8)
        mvc_bf = carve(57344, 4096, BF16, "p (m b n) -> p m b n", m=2, b=8)
        MG = carve(0, 34816, BF16, "p (c n) -> p c n", c=8)
        gfin_t = xb[:, 0:4, :].bitcast(F32).rearrange("p c n -> p (c n)")
        xr = [kv32[:, 4 * i:4 * i + 4, :].rearrange("p c n -> p (c n)") for i in range(2)]
        y1s = [carve(34816 + 4096 * i, 4096, F32) for i in range(2)]
        yo = [carve(43008 + 4096 * i, 4096, F32) for i in range(2)]
        ssq_t = sb("ssq_t", [128, 4], F32)
        MGs = sb("MGs", [128, 8, 128], BF16)

        def mm(out, lhsT, rhs, start, stop):
            return lambda e: e.matmul(out, lhsT, rhs, start=start, stop=stop, skip_group_check=True)

        def actf(out, in_, func, scale=1.0, bias=0.0):
            return lambda e: e.activation(out=out, in_=in_, func=func, scale=scale, bias=bias)

        def tt(out, a, b, op):
            return lambda e: e.tensor_tensor(out=out, in0=a, in1=b, op=op)

        def tsm(out, a, s):
            return lambda e: e.tensor_scalar(out=out, in0=a, scalar1=s, scalar2=None, op0=ALU.mult)

        def stt(out, a, s, b, op0, op1):
            return lambda e: e.scalar_tensor_tensor(out=out, in0=a, scalar=s, in1=b, op0=op0, op1=op1)

        def cp(out, in_):
            return lambda e: e.tensor_copy(out, in_)

        def mset(ap, v):
            return lambda e: e.memset(ap, v)

        def dma(out, in_):
            return lambda e: e.dma_start(out=out, in_=in_)

        P.add("sp", dma(cst_st, cst), w=["cst_st"], dsem="d_c")
        P.add("sp", dma(cmask_st, cmask), w=["cmask_st"], dsem="d_c")
        P.add("sp", dma(cmask_s_st, cmask_s), w=["cmask_s_st"], dsem="d_c")
        P.add("sp", dma(gn_t[:], gn), w=["gn"], dsem="d_c")
        P.add("sp", dma(gm_t[:], gm), w=["gm"], dsem="d_c")
        P.add("sp", dma(selw_t[0:64, :], selw), w=["selw"], dsem="d_c")
        P.add("sp", dma(pscale_t[0:64, :], pscale), w=["pscale"], dsem="d_c")
        P.add("sp", dma(corr_t[0:64, :], corr), w=["corr"], dsem="d_c")
        tlast = P.add("sp", dma(poolw32[0:64, :], poolw), w=["poolw32"], dsem="d_c")
        P.group(["cst_st", "cmask_st", "cmask_s_st", "gn", "gm", "selw", "pscale", "corr", "poolw32"], tlast)
        P.add("sp", dma(wmst, wmsel.rearrange("(c p) n -> p c n", p=128)), w=["wmst"], dsem="d_w")
        tlast = P.add("sp", dma(memxs, memT.rearrange("(c p) n -> p c n", p=128)), w=["memxs"], dsem="d_w")
        P.group(["wmst", "memxs"], tlast)
        for hh in range(2):
            tlast = P.add("sp", dma(wst[:, 4 * hh:4 * hh + 4, :],
                                    wsel.rearrange("(c p) n -> p c n", p=128)[:, 4 * hh:4 * hh + 4, :]),
                          w=["wst%d" % hh], dsem="d_w2")
        P.group(["wst0", "wst1"], tlast)

        for i, t in enumerate([tri_bf, negones_bf, ident_bf, ones_bf, onespad_bf]):
            P.add("dve", cp(t[:], cst_st[:, i, :]), r=["cst_st"], w=["const%d" % i])
        P.add("dve", cp(cmask_bf[:], cmask_st), r=["cmask_st"], w=["cmask_bf"])
        P.add("dve", cp(cmask_s_bf[:], cmask_s_st), r=["cmask_s_st"], w=["cmask_s_bf"])
        P.add("dve", mset(one32[:], 1.0), w=["one32"])
        P.add("dve", cp(poolw_bf[0:64, :], poolw32[0:64, :]), r=["poolw32"], w=["poolw_bf"])
        WG = ["Wg%d" % c for c in range(8)]
        P.add("dve", mset(Wmf, 0.0), w=["Wmf"])
        for c in range(8):
            P.add("dve", tsm(Wmt[:, c, :], wmst[:, c, :], gm_t[:, c:c + 1]), r=["wmst", "gm"], rw=["Wmt"])
        P.add("dve", cp(Wmf[:, :, 64:128], Wmt[:, :, 0:64]), r=["Wmt"], rw=["Wmf"])
        P.add("dve", mset(mvpad_bf[:], 0.0), w=["mvpad"])
        for i in range(2):
            P.add("pool", dma(WO[:, 4 * i:4 * i + 4, :], wout.rearrange("(c p) n -> p c n", p=128)[:, 4 * i:4 * i + 4, :]),
                  rw=["WO"], dsem="d_wo")
        P.group_last(["WO"])

        pbi = [0]

        def next_pb():
            i = pbi[0]
            pbi[0] ^= 1
            return i

        def rms_stats(src, srckeys, n, psub, nsub):
            for c in range(8):
                P.add("dve", cp(xb[:, c, 0:n], src[:, c, 0:n]), r=srckeys, w=["xb%d" % c])
            b = next_pb()
            for c in range(8):
                P.add("dve", tt(xsq[:, c, 0:n], src[:, c, 0:n], src[:, c, 0:n], ALU.mult),
                      r=srckeys, w=["xsq%d" % c])
                P.add("pe", mm(PB[b][:, 0:n], ones_bf[:], xsq[:, c, 0:n], c == 0, c == 7),
                      r=["xsq%d" % c, "const3"], rw=["PB%d" % b])
            P.add("act", actf(lnms[:, 0:n], PB[b][:, 0:n], AF.Ln, scale=1.0 / D, bias=EPS),
                  r=["PB%d" % b], w=["lnms"])
            P.add("act", actf(rstd_bc[:, 0:n], lnms[:, 0:n], AF.Exp, scale=-0.5), r=["lnms"], w=["rstd_bc"])
            b = next_pb()
            for s in range(nsub):
                P.add("pe", mm(PB[b][0:psub, s:s + 1], rstd_bc[0:1, s * psub:(s + 1) * psub], one32[0:1, 0:1],
                               s == 0, s == nsub - 1),
                      r=["rstd_bc", "one32"], rw=["PB%d" % b])
            P.add("dve", cp(rstd_tm[0:psub, 0:nsub], PB[b][0:psub, 0:nsub]), r=["PB%d" % b], w=["rstd_tm"])

        XB = ["xb%d" % c for c in range(8)]

        rms_stats(memxs, ["memxs"], 256, 128, 2)
        b = next_pb()
        for c in range(8):
            P.add("pe", mm(PB[b][:, 0:256], Wmf[:, c, :], xb[:, c, 0:256], c == 0, c == 7),
                  r=["Wmf", "xb%d" % c], rw=["PB%d" % b])
        P.add("dve", tt(mkT_bf[64:128, :], PB[b][64:128, 0:256], rstd_bc[64:128, 0:256], ALU.mult),
              r=["PB%d" % b, "rstd_bc"], w=["mkT_bf"])
        b = next_pb()
        for m in range(2):
            for c in range(8):
                P.add("pe", mm(PB[b][:, m * 128:(m + 1) * 128], xb[:, c, m * 128:(m + 1) * 128], Wmt[:, c, :],
                               c == 0, c == 7),
                      r=["Wmt", "xb%d" % c], rw=["PB%d" % b])
        for m in range(2):
            P.add("dve", tsm(kv32[:, m, 0:128], PB[b][:, m * 128:(m + 1) * 128], rstd_tm[:, m:m + 1]),
                  r=["PB%d" % b, "rstd_tm"], rw=["kv32"])
        P.add("dve", cp(mvpad_bf[:, :, 64:128], kv32[:, 0:2, 64:128]), r=["kv32"], rw=["mvpad"])
        P.add("pool", dma(memkv_out.rearrange("(m p) f -> p m f", p=128), kv32[:, 0:2, 0:128]),
              r=["kv32"], dsem="o_misc", final=True)

        for c in range(8):
            P.add("dve", tsm(Wg[:, c, :], wst[:, c, :], gn_t[:, c:c + 1]),
                  r=["wst%d" % (c // 4), "gn"], w=["Wg%d" % c])

        state = {"blk": 0}
        PROMPT_ARENA_KEYS = ["kT%d" % i for i in range(NT)] + ["V%d" % i for i in range(NT)]

        SETUP_ARENA_KEYS = ["Wmt", "Wmf", "memxs"]

        def prefetch_chunk(ch):
            if ch < 0:
                return
            slot = ch % 4
            ov = PROMPT_ARENA_KEYS if ch in (13, 12) else []
            P.add("pool", dma(kbf[slot], kcT[ch].rearrange("p (b n) -> p b n", b=8)),
                  w=["kbf%d" % slot] + ov, dsem="d_kv%d" % slot)
            P.add("pool", dma(vbf[slot], vc[ch].rearrange("p (l b n) -> p l b n", l=2, b=8)),
                  w=["vbf%d" % slot] + ov, dsem="d_kv%d" % slot)
            P.group_last(["kbf%d" % slot, "vbf%d" % slot])

        def load_x(ti):
            t0 = ti * TW
            for hh in range(2):
                P.add("sp", dma(xs[:, 4 * hh:4 * hh + 4, :],
                                xT.rearrange("(c p) t -> p c t", p=128)[:, 4 * hh:4 * hh + 4, t0:t0 + TW]),
                      w=["xs%d" % hh], dsem="d_x%d" % hh)

        def tile_front(ti, nxt):
            sample = (ti == NT)
            t0 = ti * TW
            par = ti % 2
            qT = qT2[:, par, :]
            sg1 = sg1_2[:, par, :]
            qTk, sg1k = "qT%d" % par, "sg1_%d" % par
            psub, nsub = (64, 8) if sample else (128, 4)
            nb, L = (8, 64) if sample else (1, 512)
            Wd = 16 + L
            if sample:
                P.add("pool", dma(mkc_bf[64:128, :, :], mkcT), w=["mkc_bf"] + SETUP_ARENA_KEYS, dsem="d_h2")
                P.add("pool", dma(mvc_bf, mvc.rearrange("(m p) b f -> p m b f", p=128)), w=["mvc_bf"] + SETUP_ARENA_KEYS,
                      dsem="d_h2")
                P.group_last(["mkc_bf", "mvc_bf"])
                if "nocache" not in DEBUG_S:
                    prefetch_chunk(15)
                    prefetch_chunk(14)
            rms_stats(xs, ["xs0", "xs1"], TW, psub, nsub)
            if nxt is not None:
                load_x(nxt)

            def fm_proj(col0):
                b = next_pb()
                for c in range(8):
                    P.add("pe", mm(PB[b][:, :], Wg[:, c, col0:col0 + 128], xb[:, c, :], c == 0, c == 7),
                          r=[WG[c], XB[c]], rw=["PB%d" % b])
                return b

            def do_T0():
                b = fm_proj(0)
                P.add("dve", stt(qT, PB[b][:, :], 0.125, rstd_bc[:], ALU.mult, ALU.mult),
                      r=["PB%d" % b, "rstd_bc"], w=[qTk])

            def do_T1():
                b = fm_proj(128)
                kdst = kTs_bf if sample else kT_all[:, t0:t0 + TW]
                P.add("dve", tt(kdst, PB[b][:, :], rstd_bc[:], ALU.mult),
                      r=["PB%d" % b, "rstd_bc"], w=["kT%d" % ti] + (SETUP_ARENA_KEYS if sample else []))

            def do_T3():
                b = fm_proj(384)
                P.add("dve", tt(sg1, PB[b][:, :], rstd_bc[:], ALU.mult), r=["PB%d" % b, "rstd_bc"], w=[sg1k])

            def do_T4():
                b = fm_proj(512)
                if sample:
                    P.add("dve", mset(ubuf[0:64, :], 0.0), w=["ubuf"])
                    ub3 = ubuf[0:64, :].rearrange("p (b k) -> p b k", b=8)
                    P.add("sp", dma(ub3[:, :, 1:16], hist), rw=["ubuf"], dsem="d_h")
                    P.add("dve", tt(ub3[:, :, 16:80], PB[b][0:64, :].rearrange("p (b k) -> p b k", b=8),
                                    rstd_bc[0:64, :].rearrange("p (b k) -> p b k", b=8), ALU.mult),
                          r=["PB%d" % b, "rstd_bc"], rw=["ubuf"])
                else:
                    if ti == 0:
                        P.add("dve", mset(ubuf[0:64, 0:16], 0.0), rw=["ubuf"])
                    P.add("dve", tt(ubuf[0:64, 16:528], PB[b][0:64, :], rstd_bc[0:64, :], ALU.mult),
                          r=["PB%d" % b, "rstd_bc"], rw=["ubuf"])
                P.add("dve", stt(qx[64:128, :], PB[b][64:128, :], 0.125, rstd_bc[64:128, :], ALU.mult, ALU.mult),
                      r=["PB%d" % b, "rstd_bc"], w=["qx"])

            def do_T5():
                b = fm_proj(640)
                P.add("dve", tt(sg2[:], PB[b][:, :], rstd_bc[:], ALU.mult), r=["PB%d" % b, "rstd_bc"], w=["sg2"])


            def do_TM():
                for s in range(nsub):
                    if s % 2 == 0:
                        b = next_pb()
                    co = (s % 2) * 256
                    for c in range(8):
                        P.add("pe", mm(PB[b][0:psub, co:co + 256], xb[:, c, s * psub:(s + 1) * psub], Wg[:, c, 128:384],
                                       c == 0, c == 7),
                              r=[WG[c], XB[c]], rw=["PB%d" % b])
                    P.add("dve", tsm(kv32[0:psub, s, :], PB[b][0:psub, co:co + 256], rstd_tm[0:psub, s:s + 1]),
                          r=["PB%d" % b, "rstd_tm"], rw=["kv32"])
                if sample:
                    P.add("dve", cp(Vs_bf[0:64, :, :], kv32[0:64, :, 128:256]), r=["kv32"], w=["Vs32"] + SETUP_ARENA_KEYS)
                    P.add("pool", dma(k_out[t0:t0 + TW, :].rearrange("(s p) f -> p s f", p=64), kv32[0:64, :, 0:128]),
                          r=["kv32"], dsem="o_kv", final=True)
                    P.add("pool", dma(v_out[t0:t0 + TW, :].rearrange("(s p) f -> p s f", p=64), kv32[0:64, :, 128:256]),
                          r=["kv32"], dsem="o_kv", final=True)
                else:
                    P.add("dve", cp(V_all[:, 4 * ti:4 * ti + 4, :], kv32[:, 0:4, 128:256]),
                          r=["kv32"], w=["V%d" % ti])
                    P.add("pool", dma(k_out[t0:t0 + TW, :].rearrange("(s p) f -> p s f", p=128), kv32[:, 0:4, 0:128]),
                          r=["kv32"], dsem="o_kv", final=True)
                    P.add("pool", dma(v_out[t0:t0 + TW, :].rearrange("(s p) f -> p s f", p=128), kv32[:, 0:4, 128:256]),
                          r=["kv32"], dsem="o_kv", final=True)


            def silu_gate(g, tmp, key, tkey):
                P.add("act", actf(tmp[:], g, AF.Exp, scale=-1.0), r=[key], w=[tkey])
                P.add("act", actf(tmp[:], tmp[:], AF.Ln, bias=1.0), rw=[tkey])
                P.add("act", actf(tmp[:], tmp[:], AF.Exp, scale=-1.0), rw=[tkey])
                P.add("dve", tt(g, g, tmp[:], ALU.mult), r=[tkey], rw=[key])

            if sample:
                U = ubuf[0:64, :].rearrange("p (b k) -> p b k", b=8)
                A = sa[0:64, :].rearrange("p (b k) -> p b k", b=8)
                B_ = sbb[0:64, :].rearrange("p (b k) -> p b k", b=8)
                PA = pacc[0:64, :].rearrange("p (b k) -> p b k", b=8)
                PO = pooled[0:64, :].rearrange("p (b k) -> p b k", b=8)
                sl = lambda X, a, bb_: X[:, :, a:bb_]
            else:
                U = ubuf[0:64, 0:528]
                A = sa[0:64, 0:528]
                B_ = sbb[0:64, 0:528]
                PA = pacc[0:64, :]
                PO = pooled[0:64, :]
                sl = lambda X, a, bb_: X[:, a:bb_]
            full = lambda X: X[:, :, :] if sample else X[:, :]
            def pool_part0():
                P.add("dve", tt(sl(A, 1, Wd), sl(U, 1, Wd), sl(U, 0, Wd - 1), ALU.add), r=["ubuf"], w=["sa"])
                P.add("dve", tsm(full(PA), sl(A, 16, Wd), selw_t[0:64, 0:1]), r=["sa", "selw"], w=["pacc"])

            def pool_part1():
                P.add("dve", tt(sl(B_, 3, Wd), sl(A, 3, Wd), sl(A, 1, Wd - 2), ALU.add), r=["sa"], w=["sbb"])
                P.add("dve", stt(full(PA), sl(B_, 16, Wd), selw_t[0:64, 1:2], full(PA), ALU.mult, ALU.add),
                      r=["sbb", "selw"], rw=["pacc"])

            def pool_part2():
                P.add("dve", tt(sl(A, 7, Wd), sl(B_, 7, Wd), sl(B_, 3, Wd - 4), ALU.add), r=["sbb"], w=["sa"])
                P.add("dve", stt(full(PA), sl(A, 16, Wd), selw_t[0:64, 2:3], full(PA), ALU.mult, ALU.add),
                      r=["sa", "selw"], rw=["pacc"])

            def pool_part3():
                P.add("dve", tt(sl(B_, 15, Wd), sl(A, 15, Wd), sl(A, 7, Wd - 8), ALU.add), r=["sa"], w=["sbb"])
                P.add("dve", stt(full(PA), sl(B_, 16, Wd), selw_t[0:64, 3:4], full(PA), ALU.mult, ALU.add),
                      r=["sbb", "selw"], rw=["pacc"])
                if ti == 0:
                    P.add("dve", tt(pacc[0:64, :], pacc[0:64, :], corr_t[0:64, :], ALU.mult), r=["corr"], rw=["pacc"])
                P.add("dve", tt(full(PO), full(PA), sl(U, 16, Wd), ALU.subtract), r=["pacc", "ubuf"], w=["pooled"])
                if sample:
                    P.add("pool", dma(u_out[:, 15:135].rearrange("p (b k) -> p b k", b=8), U[:, :, 65:80]),
                          r=["ubuf"], dsem="o_misc", final=True)
                else:
                    if ti == NT - 1:
                        P.add("pool", dma(u_out[:, 0:15], ubuf[0:64, 513:528]), r=["ubuf"], dsem="o_misc", final=True)
                    P.add("dve", cp(ubuf[0:64, 0:16], ubuf[0:64, 512:528]), rw=["ubuf"])

            def pool_mm():
                b = next_pb()
                P.add("pe", mm(PB[b][0:64, :], poolw_bf[0:64, :], pooled[0:64, :], True, True),
                      r=["poolw_bf", "pooled"], rw=["PB%d" % b])
                P.add("dve", tsm(O2[0:64, :], PB[b][0:64, :], pscale_t[0:64, 0:1]), r=["PB%d" % b, "pscale"], rw=["O2"])


            def do_XA():
                bs = [next_pb(), next_pb()]
                for m in range(2):
                    if sample:
                        for bb in range(8):
                            P.add("pe", mm(PB[bs[m]][:, bb * 64:(bb + 1) * 64], mkc_bf[64:128, bb, m * 128:(m + 1) * 128],
                                           qx[64:128, bb * 64:(bb + 1) * 64], bb == 0, bb == 7),
                                  r=["mkc_bf", "qx"], rw=["PB%d" % bs[m]])
                    else:
                        P.add("pe", mm(PB[bs[m]][:, :], mkT_bf[64:128, m * 128:(m + 1) * 128], qx[64:128, :], True, True),
                              r=["mkT_bf", "qx"], rw=["PB%d" % bs[m]])
                    P.add("act", actf(pT[:, m, :], PB[bs[m]][:, :], AF.Exp), r=["PB%d" % bs[m]], w=["pT%d" % m])
                bo, bd = bs
                for m in range(2):
                    if sample:
                        for bb in range(8):
                            P.add("pe", mm(PB[bo][:, bb * 64:(bb + 1) * 64], mvc_bf[:, m, bb, :],
                                           pT[:, m, bb * 64:(bb + 1) * 64], (m == 0 and bb == 0), (m == 1 and bb == 7)),
                                  r=["mvc_bf", "pT%d" % m], rw=["PB%d" % bo])
                    else:
                        P.add("pe", mm(PB[bo][:, :], mvpad_bf[:, m, :], pT[:, m, :], m == 0, m == 1),
                              r=["mvpad", "pT%d" % m], rw=["PB%d" % bo])
                for m in range(2):
                    P.add("pe", mm(PB[bd][:, :], onespad_bf[:], pT[:, m, :], m == 0, m == 1),
                          r=["const4", "pT%d" % m], rw=["PB%d" % bd])
                P.add("act", actf(rden[64:128, :], PB[bd][64:128, :], AF.Ln), r=["PB%d" % bd], w=["rden"])
                P.add("act", actf(rden[64:128, :], rden[64:128, :], AF.Exp, scale=-1.0), rw=["rden"])
                P.add("dve", tt(O2[64:128, :], PB[bo][64:128, :], rden[64:128, :], ALU.mult),
                      r=["PB%d" % bo, "rden"], rw=["O2"])


            do_T4()
            do_T5()
            pool_part0()
            do_T3()
            silu_gate(sg2[:], tmpb, "sg2", "tmpb")
            pool_part1()
            do_T0()
            pool_part2()
            silu_gate(sg1, tmpa, sg1k, "tmpa")
            do_T1()
            pool_part3()
            do_TM()
            pool_mm()
            do_XA()
            P.add("dve", tt(Mt[:, par, 1, :], O2[:], sg2[:], ALU.mult), r=["O2", "sg2"], rw=["Mt%d" % par])

        def tile_attn(ti, bg):
            sample = (ti == NT)
            par = ti % 2
            qT = qT2[:, par, :]
            sg1 = sg1_2[:, par, :]
            qTk, sg1k = "qT%d" % par, "sg1_%d" % par
            blocks = []
            if sample:
                if "nonew" not in DEBUG_S:
                    blocks.append(dict(kind="new", c0=0, kp=64))
                if "nocache" not in DEBUG_S:
                    for kb in range(31, -1, -1):
                        blocks.append(dict(kind="cache", kb=kb, c0=0, kp=128))
            else:
                for kb in range(4 * ti + 3, -1, -1):
                    c0 = 128 * (kb - 4 * ti) if kb >= 4 * ti else 0
                    blocks.append(dict(kind="prompt", kb=kb, c0=c0, kp=128, diag=(kb >= 4 * ti)))
            NB = len(blocks)
            for i, bl in enumerate(blocks):
                bl["first"] = (i == 0)
                bl["last"] = (i == NB - 1)
            P.add("dve", mset(Sbf[:].bitcast(F32), 0.0), w=["Sbf0", "Sbf1"])
            gbase = state["blk"]
            state["blk"] += NB

            def zi(n):
                return (gbase + n) % 2

            def zkeys(n):
                return ["ZZ%d_0" % zi(n), "ZZ%d_1" % zi(n)]

            def ZZv(n):
                return ZZ[zi(n)].rearrange("p (h f) -> p h f", h=2)

            def sl2(n):
                return (gbase + n) % 2

            Ev = Ebuf.rearrange("p s (h f) -> p s h f", h=2)
            Lv = Lp.rearrange("p s (h f) -> p s h f", h=2)
            Wv = Wbuf.rearrange("p s (h f) -> p s h f", h=2)
            Sv = Sbf.rearrange("p s (h f) -> p s h f", h=2)

            if sample and "nocache" not in DEBUG_S:
                for ch0 in (13, 12):
                    prefetch_chunk(ch0)

            def addZ(n):
                bl = blocks[n]
                c0, kp = bl["c0"], bl["kp"]
                zz = ZZ[zi(n)]
                for h in range(2):
                    zk = ["ZZ%d_%d" % (zi(n), h)]
                    o = h * 512
                    if bl["kind"] == "prompt":
                        kb = bl["kb"]
                        P.add("pe", mm(zz[:, o + c0:o + 512], kT_all[64 * h:64 * h + 64, kb * 128:(kb + 1) * 128],
                                       qT[64 * h:64 * h + 64, c0:512], True, True),
                              r=["kT%d" % (kb // 4), qTk], w=zk)
                        if bl["diag"]:
                            P.add("pe", mm(zz[:, o + c0:o + c0 + 128], ident_bf[:], cmask_bf[:], False, True),
                                  r=["const2", "cmask_bf"], rw=zk)
                    elif bl["kind"] == "new":
                        for bb in range(8):
                            P.add("pe", mm(zz[0:64, o + bb * 64:o + (bb + 1) * 64],
                                           kTs_bf[64 * h:64 * h + 64, bb * 64:(bb + 1) * 64],
                                           qT[64 * h:64 * h + 64, bb * 64:(bb + 1) * 64], bb == 0, True),
                                  r=["kT%d" % NT, qTk], rw=zk)
                        P.add("pe", mm(zz[0:64, o:o + 512], ident_bf[0:64, 0:64], cmask_s_bf[0:64, :], False, True),
                              r=["const2", "cmask_s_bf"], rw=zk)
                    else:
                        kb = bl["kb"]
                        slot, loc = (kb // 2) % 4, kb % 2
                        for bb in range(8):
                            P.add("pe", mm(zz[:, o + bb * 64:o + (bb + 1) * 64],
                                           kbf[slot][64 * h:64 * h + 64, bb, loc * 128:(loc + 1) * 128],
                                           qT[64 * h:64 * h + 64, bb * 64:(bb + 1) * 64], bb == 0, True),
                                  r=["kbf%d" % slot, qTk], rw=zk)

            def addP1(n):
                bl = blocks[n]
                c0, kp = bl["c0"], bl["kp"]
                P.add("act", actf(Ev[0:kp, sl2(n), :, c0:512], ZZv(n)[0:kp, :, c0:512], AF.Exp),
                      r=zkeys(n), w=["E%d" % sl2(n)])

            def addP2(n):
                bl = blocks[n]
                c0, kp = bl["c0"], bl["kp"]
                P.add("act", actf(Lv[0:kp, sl2(n), :, c0:512], Ev[0:kp, sl2(n), :, c0:512], AF.Ln, bias=1.0),
                      r=["E%d" % sl2(n)], w=["Lp%d" % sl2(n)])

            def addSA(n):
                bl = blocks[n]
                if bl["last"]:
                    return
                c0, kp = bl["c0"], bl["kp"]
                cur, nx = sl2(n), sl2(n + 1)
                P.add("dve", tt(Sv[0:kp, nx, :, c0:512], Sv[0:kp, cur, :, c0:512], Lv[0:kp, sl2(n), :, c0:512], ALU.add),
                      r=["Lp%d" % sl2(n), "Sbf%d" % cur], rw=["Sbf%d" % nx])

            def addNO(n):
                bl = blocks[n]
                if bl["first"]:
                    return
                c0, kp = bl["c0"], bl["kp"]
                zz = ZZ[zi(n)]
                ss = sl2(n)
                for h in range(2):
                    o = h * 512
                    P.add("pe", mm(zz[0:kp, o + c0:o + 512], negones_bf[:, 0:kp], Sv[:, ss, h, c0:512], False, False),
                          r=["const1", "Sbf%d" % ss], rw=["ZZ%d_%d" % (zi(n), h)])

            def addTRI(n):
                bl = blocks[n]
                c0, kp = bl["c0"], bl["kp"]
                zz = ZZ[zi(n)]
                for h in range(2):
                    o = h * 512
                    P.add("pe", mm(zz[0:kp, o + c0:o + 512], tri_bf[0:kp, 0:kp], Lv[0:kp, sl2(n), h, c0:512], False, True),
                          r=["const0", "Lp%d" % sl2(n)], rw=["ZZ%d_%d" % (zi(n), h)])

            def addP3(n):
                bl = blocks[n]
                c0, kp = bl["c0"], bl["kp"]
                P.add("act", actf(Wv[0:kp, sl2(n), :, c0:512], ZZv(n)[0:kp, :, c0:512], AF.Exp),
                      r=zkeys(n), w=["W%d" % sl2(n)])

            def addAV(n):
                bl = blocks[n]
                c0, kp = bl["c0"], bl["kp"]
                for h in range(2):
                    acc = ACC[h]
                    ak = "ACC%d" % h
                    if bl["kind"] == "prompt":
                        kb = bl["kb"]
                        P.add("pe", mm(acc[:, c0:512], V_all[:, kb, :], Wv[:, sl2(n), h, c0:512], bl["first"], bl["last"]),
                              r=["V%d" % (kb // 4), "W%d" % sl2(n)], rw=[ak])
                    elif bl["kind"] == "new":
                        for bb in range(8):
                            P.add("pe", mm(acc[:, bb * 64:(bb + 1) * 64], Vs_bf[0:64, bb, :],
                                           Wv[0:64, sl2(n), h, bb * 64:(bb + 1) * 64], bb == 0, False),
                                  r=["Vs32", "W%d" % sl2(n)], rw=[ak])
                    else:
                        kb = bl["kb"]
                        slot, loc = (kb // 2) % 4, kb % 2
                        for bb in range(8):
                            P.add("pe", mm(acc[:, bb * 64:(bb + 1) * 64], vbf[slot][:, loc, bb, :],
                                           Wv[:, sl2(n), h, bb * 64:(bb + 1) * 64], False, bl["last"]),
                                  r=["vbf%d" % slot, "W%d" % sl2(n)], rw=[ak])
                if bl["kind"] == "cache" and bl["kb"] % 2 == 0:
                    prefetch_chunk(bl["kb"] // 2 - 4)

            nodes = P.plan_bg(bg)
            cnt = {}
            for nd in nodes:
                cnt[nd["eng"]] = cnt.get(nd["eng"], 0) + 1
            base = {"pe": 5, "dve": 2, "act": 1, "pool": 2, "sp": 2}
            nit = max(NB - 2, 1)
            caps = {e: max(base[e], -(-cnt.get(e, 0) // nit)) for e in base}
            if NB > 0:
                addZ(0)
            for n in range(0, NB + 1):
                if n < NB:
                    addP1(n)
                    addNO(n)
                if n - 1 >= 0:
                    addP3(n - 1)
                if n + 1 < NB:
                    addZ(n + 1)
                if n < NB:
                    addP2(n)
                    addSA(n)
                    addTRI(n)
                if n - 1 >= 0:
                    addAV(n - 1)
                if n < NB:
                    P.issue_bg(nodes, n, caps)
                if sample and n == 14:
                    emit_prompt_gathers()
            P.issue_bg(nodes, NB + 1, caps, flush=True)

            slot = par
            P.add("dve", tt(Mt[0:64, slot, 0, :], ACC[0][0:64, :], sg1[0:64, :], ALU.mult),
                  r=["ACC0", sg1k], rw=["Mt%d" % slot])
            P.add("dve", tt(Mt[64:128, slot, 0, :], ACC[1][64:128, :], sg1[64:128, :], ALU.mult),
                  r=["ACC1", sg1k], rw=["Mt%d" % slot])
            if DEBUG_DUMP:
                P.add("dve", cp(tmpa[0:64, :], ACC[0][0:64, :]), r=["ACC0"], rw=["tmpa"])
                P.add("dve", cp(tmpa[64:128, :], ACC[1][64:128, :]), r=["ACC1"], rw=["tmpa"])
                P.add("pool", dma(dbg[ti, 0], tmpa[:]), r=["tmpa"], dsem="o_dbg", final=True)
                P.add("pool", dma(dbg[ti, 1], O2[:]), r=["O2"], dsem="o_dbg", final=True)
                P.add("pool", dma(dbg[ti, 2], sg1), r=[sg1k], dsem="o_dbg", final=True)
                P.add("pool", dma(dbg[ti, 3], sg2[:]), r=["sg2"], dsem="o_dbg", final=True)
                P.add("pool", dma(dbgM[ti].rearrange("t p f -> p t f"), Mt[:, slot, :, :]), r=["Mt%d" % slot], dsem="o_dbg", final=True)
            if sample:
                ccs = cc_in_s.rearrange("(j t p) f -> j t p f", j=4, t=2)
                for t in range(2):
                    P.add("pool", dma(ccs[:, t, :, :].rearrange("j p f -> p j f"),
                                      Mt[:, slot, t, :].rearrange("p (j f) -> p j f", j=4)),
                          r=["Mt%d" % slot], rw=["cc_in_s"], dsem="o_m%d" % slot)
                P.add("pool", lambda e: e.collective_compute("AllGather", ALU.bypass, replica_groups=RG,
                                                             ins=[cc_in_s], outs=[cc_out_s]),
                      r=["cc_in_s"], w=["cc_out_s"], dsem="d_cc", inc=1)
            else:
                g, cc0 = ti // 4, (ti % 4) * 512
                ccv = cc_in.rearrange("(g t p) f -> g t p f", g=4, t=2)
                P.add("pool", dma(ccv[g, :, :, cc0:cc0 + 512].rearrange("t p f -> p t f"), Mt[:, slot, :, :]),
                      r=["Mt%d" % slot], rw=["cc_in%d" % g], dsem="o_m%d" % slot)
                if ti % 4 == 3 and not DEBUG_NO_P4:
                    P.add("pool", (lambda g_: (lambda e: e.collective_compute(
                        "AllGather", ALU.bypass, replica_groups=RG,
                        ins=[cc_in[g_ * 256:(g_ + 1) * 256, :]], outs=[cc_out[g_ * 1024:(g_ + 1) * 1024, :]])))(g),
                          r=["cc_in%d" % g], w=["cc_out%d" % g], dsem="d_cc", inc=1)

        gath = {}

        def gather_dma(ch, smp):
            def fn(e):
                if "my" not in gath:
                    gath["my"] = e.partition_id() % 4
                if smp:
                    view = cc_out_s.rearrange("(r j t p) f -> j p r t f", r=4, j=4, t=2)
                    src = view[bass.ds(gath["my"], 1)][0]
                    return e.dma_start(out=MGs[:, ch, :], in_=src[:, ch // 2, ch % 2, :])
                view = cc_out.rearrange("(g r t p) f -> g p r t f", g=4, r=4, t=2)
                src = view[bass.ds(gath["my"], 1)][0]
                return e.dma_start(out=MG[:, ch, 0:2048], in_=src[:, ch // 2, ch % 2, :])
            return fn

        gstate = {"done": False}

        def emit_prompt_gathers():
            if gstate["done"] or DEBUG_NO_P4:
                return
            gstate["done"] = True
            mg_reads = ["kT%d" % i for i in range(NT)] + ["xs0", "xs1"]
            cco = ["cc_out%d" % g for g in range(4)]
            tl = None
            for ch in range(8):
                tl = P.add("pool", gather_dma(ch, False), r=cco, w=["MGc%d" % ch] + (mg_reads + ["MG0", "MG1"] if ch == 0 else []),
                           dsem="d_g")
            P.group(["MG0", "MG1"], tl)

        P4_SCRATCH = ["kbf0", "kbf1", "vbf0", "vbf1"] + ["V%d" % i for i in range(NT)]

        ntiles = NT if DEBUG_TILES is None else DEBUG_TILES
        tiles = list(range(ntiles)) + ([] if DEBUG_NO_SAMPLE else [NT])
        if tiles:
            load_x(tiles[0])
            tile_front(tiles[0], tiles[1] if len(tiles) > 1 else None)
        for i, ti in enumerate(tiles):
            bg = []
            if i + 1 < len(tiles):
                P.begin_defer()
                tile_front(tiles[i + 1], tiles[i + 2] if i + 2 < len(tiles) else None)
                bg = P.end_defer()
            tile_attn(ti, bg)

        if not DEBUG_NO_P4:
            emit_prompt_gathers()

            def sample_gathers():
                tl = None
                for ch in range(8):
                    tl = P.add("pool", gather_dma(ch, True), r=["cc_out_s"], w=["MGsc%d" % ch] + (["MGs"] if ch == 0 else []),
                               dsem="d_gs")
                P.group(["MGs"], tl)
            if DEBUG_DUMP:
                P.add("pool", dma(dbgMG, MG), r=["MG0", "MG1"], dsem="o_dbg", final=True)
            P.add("sp", dma(gfin_t, gfin), w=XB[0:4], dsem="d_h")

            nst = OUTTOK // 128
            for st in range(nst):
                slot = st % 2
                smp_st = (st == nst - 1)
                if smp_st and not DEBUG_NO_SAMPLE:
                    sample_gathers()
                P.add("sp", dma(xr[slot], xres[st * 128:(st + 1) * 128, :]),
                      w=["xr%d" % slot] + (["kv32"] if st < 2 else []), dsem="d_x%d" % slot)
                y1 = y1s[slot]
                y1k = "y1_%d" % slot
                for nh in range(2):
                    zb = ZZ[st % 2][:, nh * 512:(nh + 1) * 512]
                    zk = "ZZ%d_%d" % (st % 2, nh)
                    for ch in range(8):
                        lhs = MGs[:, ch, :] if smp_st else MG[:, ch, st * 128:(st + 1) * 128]
                        P.add("pe", mm(zb, lhs, WO[:, ch, nh * 512:(nh + 1) * 512], ch == 0, ch == 7),
                              r=[("MGs" if smp_st else "MG%d" % (ch // 4)), "WO"], rw=[zk])
                    P.add("dve", tt(y1[:, nh * 512:(nh + 1) * 512], zb, xr[slot][:, nh * 512:(nh + 1) * 512], ALU.add),
                          r=[zk, "xr%d" % slot], rw=[y1k], w=(P4_SCRATCH if st < 2 else []))
                if DEBUG_DUMP:
                    P.add("pool", dma(dbgY1[st * 128:(st + 1) * 128, :], y1), r=[y1k], dsem="o_dbg", final=True)
                P.add("act", actf(yo[slot], y1, AF.Square), r=[y1k], w=["yo%d" % slot] + (P4_SCRATCH if st < 2 else []))
                P.add("dve", lambda e, sl_=slot: e.reduce_sum(out=ssq_t[:, 0:1], in_=yo[sl_], axis=mybir.AxisListType.X),
                      r=["yo%d" % slot], w=["ssq"])
                P.add("act", actf(ssq_t[:, 1:2], ssq_t[:, 0:1], AF.Ln, scale=1.0 / D, bias=EPS), r=["ssq"], w=["ssq1"])
                P.add("act", actf(ssq_t[:, 2:3], ssq_t[:, 1:2], AF.Exp, scale=-0.5), r=["ssq1"], w=["ssq2"])
                P.add("dve", stt(yo[slot], y1, ssq_t[:, 2:3], gfin_t, ALU.mult, ALU.mult),
                      r=[y1k, "ssq2"] + XB[0:4], rw=["yo%d" % slot])
                P.add("act", dma(y_out[st * 128:(st + 1) * 128, :], yo[slot]), r=["yo%d" % slot],
                      dsem="o_y%d" % slot, final=True)


        semnames = P.finalize()
        sems = {n: es.enter_context(nc.semaphore(n)) for n in semnames}
        with nc.Block() as block:
            @block.sync
            def _(e):
                P.emit("sp", e, sems)

            @block.tensor
            def _(e):
                P.emit("pe", e, sems)

            @block.scalar
            def _(e):
                P.emit("act", e, sems)

            @block.vector
            def _(e):
                P.emit("dve", e, sems)

            @block.gpsimd
            def _(e):
                P.emit("pool", e, sems, final_waits=True)
    return nc


_NC_CACHE = {}


def _consts():
    k = np.arange(128)[:, None]
    m = np.arange(128)[None, :]
    tri = np.where(k >= m, -1.0, 0.0).astype(np.float32)
    negones = -np.ones((128, 128), np.float32)
    ident = np.eye(128, dtype=np.float32)
    ones = np.ones((128, 128), np.float32)
    onespad = np.concatenate([np.zeros((128, 64), np.float32), np.ones((128, 64), np.float32)], axis=1)
    cst = np.stack([tri, negones, ident, ones, onespad], axis=1)
    cmask = np.where(k >= m, NEG, 0.0).astype(np.float32)
    k64 = np.arange(128)[:, None]
    f = np.arange(512)[None, :] % 64
    cmask_s = np.where(k64 >= f, NEG, 0.0).astype(np.float32)
    return np.ascontiguousarray(cst), cmask, np.ascontiguousarray(cmask_s)


def _prep_inputs(c, x_prompt, x_sample, cache_sb_k, cache_sb_v, state_pool, cache_mem_k, cache_mem_v, mem_prompt,
                 g_norm, w_in, pool_w, pool_scale, g_mem, w_mem_kv, w_out, g_final):
    b, j = c // 4, c % 4
    sb_ = slice(8 * b, 8 * b + 8)
    f32 = lambda a: np.ascontiguousarray(a, dtype=np.float32)
    xs_s = x_sample[sb_].reshape(512, D)
    xT = np.concatenate([x_prompt[b].T, xs_s.T], axis=1)
    cols = np.concatenate([
        np.arange(128 * j, 128 * j + 128),
        512 + np.arange(128 * j, 128 * j + 128),
        1024 + np.arange(128 * j, 128 * j + 128),
        1536 + np.arange(128 * j, 128 * j + 128),
        2048 + np.arange(64 * j, 64 * j + 64),
        2560 + np.arange(64 * j, 64 * j + 64),
        2304 + np.arange(64 * j, 64 * j + 64),
        2816 + np.arange(64 * j, 64 * j + 64)])
    wsel = w_in[0][:, cols]
    wmsel = w_mem_kv[0][:, np.concatenate([np.arange(64 * j, 64 * j + 64), 256 + np.arange(64 * j, 64 * j + 64)])]
    wins = np.array([2, 4, 8, 16], np.float32)
    selw = np.zeros((64, 4), np.float32)
    selw[:, j] = 1.0 / wins[j]
    t = np.arange(512, dtype=np.float32)
    corr = np.broadcast_to(wins[j] / np.minimum(t + 1.0, wins[j]), (64, 512))
    rows = []
    for r in range(4):
        rows.append(np.arange(128 * r, 128 * r + 128))
        rows.append(np.concatenate([512 + np.arange(64 * r, 64 * r + 64), 768 + np.arange(64 * r, 64 * r + 64)]))
    wout = w_out[0][np.concatenate(rows), :]
    xres = np.concatenate([x_prompt[b, 2048 * j:2048 * j + 2048],
                           x_sample[8 * b + 2 * j:8 * b + 2 * j + 2].reshape(128, D)], axis=0)
    kc = cache_sb_k[0, sb_, :, 2 * j:2 * j + 2, :]
    kcT = kc.reshape(8, 16, 256, 2, 64).transpose(1, 3, 4, 0, 2).reshape(16, 128, 8 * 256)
    vcs = cache_sb_v[0, sb_, :, 2 * j:2 * j + 2, :]
    vcl = vcs.reshape(8, 16, 2, 128, 128).transpose(1, 3, 2, 0, 4).reshape(16, 128, 2 * 8 * 128)
    histl = state_pool[0, sb_, :, 64 * j:64 * j + 64].transpose(2, 0, 1)
    mkcT = cache_mem_k[0, sb_, :, j, :].transpose(2, 0, 1)
    mvc = np.zeros((256, 8, 128), np.float32)
    mvc[:, :, 64:128] = cache_mem_v[0, sb_, :, j, :].transpose(1, 0, 2)
    cst, cmask, cmask_s = _consts()
    return {
        "xT": f32(xT), "wsel": f32(wsel), "wmsel": f32(wmsel), "memT": f32(mem_prompt[b].T),
        "gn": f32(g_norm[0].reshape(8, 128).T), "gm": f32(g_mem[0].reshape(8, 128).T),
        "poolw": f32(pool_w[0, j]), "pscale": f32(pool_scale[0, 64 * j:64 * j + 64].reshape(64, 1)),
        "selw": selw, "corr": f32(corr), "wout": f32(wout),
        "gfin": f32(np.broadcast_to(g_final[None, :], (128, D))), "xres": f32(xres),
        "kcT": f32(kcT), "vc": f32(vcl), "hist": f32(histl), "mkcT": f32(mkcT), "mvc": mvc,
        "cst": cst, "cmask": cmask, "cmask_s": cmask_s,
    }


def kernel(x_prompt, x_sample, cache_sb_k, cache_sb_v, state_pool, cache_mem_k, cache_mem_v, mem_prompt,
           g_norm, w_in, pool_w, pool_scale, g_mem, w_mem_kv, w_out, g_final):
    args = [np.asarray(a) for a in (x_prompt, x_sample, cache_sb_k, cache_sb_v, state_pool, cache_mem_k,
                                    cache_mem_v, mem_prompt, g_norm, w_in, pool_w, pool_scale, g_mem,
                                    w_mem_kv, w_out, g_final)]
    if "nc" not in _NC_CACHE:
        _NC_CACHE["nc"] = build_nc()
    nc = _NC_CACHE["nc"]
    in_maps = [_prep_inputs(c, *args) for c in range(NCORES)]
    res = run_bass_kernel_spmd(nc, in_maps, core_ids=list(range(NCORES)))
    R = res.results

    y_prompt = np.zeros((2, SEQ, D), np.float32)
    y_sample = np.zeros((16, 64, D), np.float32)
    sb_k_p = np.zeros((1, 2, SEQ, 8, 64), np.float32)
    sb_v_p = np.zeros((1, 2, SEQ, 8, 64), np.float32)
    pool_p = np.zeros((1, 2, 15, 256), np.float32)
    mem_k_p = np.zeros((1, 2, 256, 4, 64), np.float32)
    mem_v_p = np.zeros((1, 2, 256, 4, 64), np.float32)
    sb_k_s = np.zeros((1, 16, 64, 8, 64), np.float32)
    sb_v_s = np.zeros((1, 16, 64, 8, 64), np.float32)
    pool_s = np.zeros((1, 16, 15, 256), np.float32)
    for c in range(NCORES):
        b, j = c // 4, c % 4
        r = R[c]
        yo = np.asarray(r["y_out"])
        y_prompt[b, 2048 * j:2048 * j + 2048] = yo[0:2048]
        y_sample[8 * b + 2 * j:8 * b + 2 * j + 2] = yo[2048:2176].reshape(2, 64, D)
        ko, vo = np.asarray(r["k_out"]), np.asarray(r["v_out"])
        sb_k_p[0, b, :, 2 * j:2 * j + 2, :] = ko[0:SEQ].reshape(SEQ, 2, 64)
        sb_v_p[0, b, :, 2 * j:2 * j + 2, :] = vo[0:SEQ].reshape(SEQ, 2, 64)
        sb_k_s[0, 8 * b:8 * b + 8, :, 2 * j:2 * j + 2, :] = ko[SEQ:].reshape(8, 64, 2, 64)
        sb_v_s[0, 8 * b:8 * b + 8, :, 2 * j:2 * j + 2, :] = vo[SEQ:].reshape(8, 64, 2, 64)
        uo = np.asarray(r["u_out"])
        pool_p[0, b, :, 64 * j:64 * j + 64] = uo[:, 0:15].T
        pool_s[0, 8 * b:8 * b + 8, :, 64 * j:64 * j + 64] = uo[:, 15:135].reshape(64, 8, 15).transpose(1, 2, 0)
        mo = np.asarray(r["memkv_out"])
        mem_k_p[0, b, :, j, :] = mo[:, 0:64]
        mem_v_p[0, b, :, j, :] = mo[:, 64:128]
    return (y_prompt, y_sample, sb_k_p, sb_v_p, pool_p, mem_k_p, mem_v_p, sb_k_s, sb_v_s, pool_s)
```

```python
import contextlib
import numpy as np
import concourse.bass as bass
import concourse.mybir as mybir
from concourse.bass_utils import run_bass_kernel_spmd

F32 = mybir.dt.float32
BF16 = mybir.dt.bfloat16
AF = mybir.ActivationFunctionType
ALU = mybir.AluOpType

NCORES = 8
D = 1024
SEQ = 8192
NT = 16
TW = 512
PAST = 4096
EPS = 1e-6
NEG = -30000.0
NTOK = SEQ + 512
OUTTOK = 2048 + 128

import os
DEBUG_TILES = int(os.environ["KDBG_TILES"]) if "KDBG_TILES" in os.environ else None
DEBUG_NO_P4 = "KDBG_NOP4" in os.environ
DEBUG_NO_SAMPLE = "KDBG_NOSAMPLE" in os.environ
DEBUG_DUMP = "KDBG_DUMP" in os.environ
DEBUG_S = os.environ.get("KDBG_S", "")


class Tok:
    __slots__ = ("eng", "sem", "inc", "value", "used")

    def __init__(self, eng, sem, inc):
        self.eng, self.sem, self.inc, self.value, self.used = eng, sem, inc, None, False


NBLK_LONG = [True]


class Prog:
    ENGS = ("pe", "act", "dve", "pool", "sp")

    def __init__(self):
        self.ops = {e: [] for e in self.ENGS}
        self.writer = {}
        self.readers = {}
        self.final = []
        self.defer = None

    def begin_defer(self):
        self.defer = []

    def end_defer(self):
        d, self.defer = self.defer, None
        return d

    def run(self, lst, k):
        n = 0
        while lst and n < k:
            a = lst.pop(0)
            if a[0] == "__group__":
                self.group(a[1], self._last_tok)
            else:
                self._last_tok = self.add(*a[0], **a[1])
            n += 1

    def plan_bg(self, lst):
        nodes = []
        writer, readers = {}, {}
        for it in lst:
            if it[0] == "__group__":
                nodes[-1]["grp"] = it[1]
                for k in it[1]:
                    writer[k] = len(nodes) - 1
                    readers[k] = set()
                continue
            (eng, fn), kw = it
            deps = set()
            for k in list(kw["r"]) + list(kw["rw"]):
                if k in writer:
                    deps.add(writer[k])
            for k in list(kw["w"]) + list(kw["rw"]):
                if k in writer:
                    deps.add(writer[k])
                deps |= readers.get(k, set())
            idx = len(nodes)
            for k in kw["r"]:
                readers.setdefault(k, set()).add(idx)
            for k in list(kw["w"]) + list(kw["rw"]):
                writer[k] = idx
                readers[k] = set()
            deps.discard(idx)
            nodes.append(dict(item=it, deps=deps, eng=eng, dma=kw["dsem"] is not None, grp=None, issued=None))
        return nodes

    def issue_bg(self, nodes, n, caps, flush=False):
        used = {}
        for nd in nodes:
            if nd["issued"] is not None:
                continue
            ok = True
            for d in nd["deps"]:
                dn = nodes[d]
                if dn["issued"] is None:
                    ok = False
                    break
                if dn["eng"] == nd["eng"] and not dn["dma"]:
                    lag = 0
                elif nd["eng"] == "act" or (nd["eng"] == "pe" and NBLK_LONG[0]):
                    lag = 4 if dn["dma"] else (3 if dn["eng"] == "pool" else 1)
                else:
                    lag = 3 if dn["dma"] else 0
                if not flush and dn["issued"] + lag > n:
                    ok = False
                    break
            if not ok:
                continue
            e = nd["eng"]
            if not flush and used.get(e, 0) >= caps.get(e, 1):
                continue
            self.add(*nd["item"][0], **nd["item"][1])
            if nd["grp"]:
                self.group(nd["grp"], self._last_tok)
            nd["issued"] = n
            used[e] = used.get(e, 0) + 1

    def add(self, eng, fn, r=(), w=(), rw=(), dsem=None, extra=(), final=False, inc=None):
        if self.defer is not None:
            self.defer.append(((eng, fn), dict(r=r, w=w, rw=rw, dsem=dsem, extra=extra, final=final, inc=inc)))
            return None
        sem = dsem if dsem is not None else "p_" + eng
        tok = Tok(eng, sem, inc if inc is not None else (16 if dsem is not None else 1))
        deps = []
        for k in list(r) + list(rw):
            t = self.writer.get(k)
            if t is not None:
                deps.append(t)
        for k in list(w) + list(rw):
            t = self.writer.get(k)
            if t is not None:
                deps.append(t)
            deps.extend(self.readers.get(k, {}).values())
        deps.extend(t for t in extra if t is not None)
        for k in r:
            self.readers.setdefault(k, {})[sem] = tok
        for k in list(w) + list(rw):
            self.writer[k] = tok
            self.readers[k] = {}
        deps = [d for d in deps if not (d.eng == "pe" and eng == "pe")]
        for d in deps:
            d.used = True
        if dsem is not None:
            tok.used = True
        if final:
            tok.used = True
            self.final.append(tok)
        self.ops[eng].append((fn, deps, tok))
        self._last_tok = tok
        return tok

    def group_last(self, keys):
        if self.defer is not None:
            self.defer.append(("__group__", list(keys)))
        else:
            self.group(keys, self._last_tok)

    def group(self, keys, tok):
        for k in keys:
            self.writer[k] = tok
            self.readers[k] = {}

    def finalize(self):
        counts = {}
        for e in self.ENGS:
            for fn, deps, tok in self.ops[e]:
                if tok.used:
                    counts[tok.sem] = counts.get(tok.sem, 0) + tok.inc
                    tok.value = counts[tok.sem]
        return sorted(counts.keys())

    def emit(self, eng_name, e, sems, final_waits=False):
        waited = {}
        for fn, deps, tok in self.ops[eng_name]:
            need = {}
            for d in deps:
                if need.get(d.sem, 0) < d.value:
                    need[d.sem] = d.value
            for s, v in need.items():
                if waited.get(s, 0) < v:
                    e.wait_ge(sems[s], v)
                    waited[s] = v
            ins = fn(e)
            if tok.used:
                ins.then_inc(sems[tok.sem], tok.inc)
        if final_waits:
            need = {}
            for t in self.final:
                if need.get(t.sem, 0) < t.value:
                    need[t.sem] = t.value
            for s, v in need.items():
                e.wait_ge(sems[s], v)


def build_nc():
    nc = bass.Bass("TRN2", target_bir_lowering=False)
    P = Prog()

    def din(name, shape, dt=F32):
        return nc.dram_tensor(name, list(shape), dt, kind="ExternalInput").ap()

    def dout(name, shape, dt=F32):
        return nc.dram_tensor(name, list(shape), dt, kind="ExternalOutput").ap()

    xT = din("xT", [D, NTOK])
    wsel = din("wsel", [D, 768])
    wmsel = din("wmsel", [D, 128])
    memT = din("memT", [D, 256])
    gn = din("gn", [128, 8])
    gm = din("gm", [128, 8])
    poolw = din("poolw", [64, 64])
    pscale = din("pscale", [64, 1])
    selw = din("selw", [64, 4])
    corr = din("corr", [64, 512])
    wout = din("wout", [D, D])
    gfin = din("gfin", [128, D])
    xres = din("xres", [OUTTOK, D])
    kcT = din("kcT", [16, 128, 8 * 256])
    vc = din("vc", [16, 128, 2 * 8 * 128])
    hist = din("hist", [64, 8, 15])
    mkcT = din("mkcT", [64, 8, 256])
    mvc = din("mvc", [256, 8, 128])
    cst = din("cst", [128, 5, 128])
    cmask = din("cmask", [128, 128])
    cmask_s = din("cmask_s", [128, 512])

    k_out = dout("k_out", [NTOK, 128])
    v_out = dout("v_out", [NTOK, 128])
    u_out = dout("u_out", [64, 135])
    memkv_out = dout("memkv_out", [256, 128])
    y_out = dout("y_out", [OUTTOK, D])
    if DEBUG_DUMP:
        dbg = dout("dbg", [17, 4, 128, 512])
        dbgM = dout("dbgM", [17, 2, 128, 512], BF16)
        dbgMG = dout("dbgMG", [128, 8, 2176], BF16)
        dbgY1 = dout("dbgY1", [OUTTOK, D])

    cc_in = nc.dram_tensor("cc_in", [4 * 256, 2048], BF16, kind="Internal").ap()
    cc_out = nc.dram_tensor("cc_out", [4 * 4 * 256, 2048], BF16, kind="Internal").ap()
    cc_in_s = nc.dram_tensor("cc_in_s", [4 * 256, 128], BF16, kind="Internal").ap()
    cc_out_s = nc.dram_tensor("cc_out_s", [4 * 4 * 256, 128], BF16, kind="Internal").ap()
    RG = [[0, 1, 2, 3], [4, 5, 6, 7]]

    es = contextlib.ExitStack()
    with es:
        def sb(name, shape, dt):
            return es.enter_context(nc.sbuf_tensor(name, list(shape), dt))

        def ps(name):
            return es.enter_context(nc.psum_tensor(name, [128, 512], F32))

        ARENA_W = 16384
        arena = sb("arena", [128, ARENA_W], F32)

        def carve(off_bytes, nbytes, dt, pattern=None, **kw):
            a = arena[:, off_bytes // 4:(off_bytes + nbytes) // 4]
            if dt != F32:
                a = a.bitcast(dt)
            if pattern is not None:
                a = a.rearrange(pattern, **kw)
            return a

        Wg = sb("Wg", [128, 8, 768], BF16)
        WO = sb("WO", [128, 8, 1024], BF16)
        tri_bf = sb("tri_bf", [128, 128], BF16)
        negones_bf = sb("negones_bf", [128, 128], BF16)
        ident_bf = sb("ident_bf", [128, 128], BF16)
        ones_bf = sb("ones_bf", [128, 128], BF16)
        onespad_bf = sb("onespad_bf", [128, 128], BF16)
        cmask_bf = sb("cmask_bf", [128, 128], BF16)
        cmask_s_bf = sb("cmask_s_bf", [128, 512], BF16)
        one32 = sb("one32", [128, 1], F32)
        gn_t = sb("gn_t", [128, 8], F32)
        gm_t = sb("gm_t", [128, 8], F32)
        selw_t = sb("selw_t", [128, 4], F32)
        pscale_t = sb("pscale_t", [128, 1], F32)
        corr_t = sb("corr_t", [128, 512], F32)
        poolw32 = sb("poolw32", [128, 64], F32)
        poolw_bf = sb("poolw_bf", [128, 64], BF16)
        mkT_bf = sb("mkT_bf", [128, 256], BF16)
        mvpad_bf = sb("mvpad_bf", [128, 2, 128], BF16)
        xb = sb("xb", [128, 8, 512], BF16)
        xsq = sb("xsq", [128, 8, 512], BF16)
        lnms = sb("lnms", [128, 512], F32)
        rstd_bc = sb("rstd_bc", [128, 512], F32)
        rstd_tm = sb("rstd_tm", [128, 8], F32)
        qT2 = sb("qT2", [128, 2, 512], BF16)
        sg1_2 = sb("sg1_2", [128, 2, 512], F32)
        sg2 = sb("sg2", [128, 512], F32)
        tmpa = sb("tmpa", [128, 512], F32)
        tmpb = sb("tmpb", [128, 512], F32)
        qx = sb("qx", [128, 512], BF16)
        kv32 = sb("kv32", [128, 8, 256], F32)
        ubuf = sb("ubuf", [128, 640], F32)
        sa = sb("sa", [128, 640], F32)
        sbb = sb("sbb", [128, 640], F32)
        pacc = sb("pacc", [128, 512], F32)
        pooled = sb("pooled", [128, 512], BF16)
        pT = sb("pT", [128, 2, 512], BF16)
        rden = sb("rden", [128, 512], F32)
        O2 = sb("O2", [128, 512], F32)
        Ebuf = sb("Ebuf", [128, 2, 1024], F32)
        Lp = sb("Lp", [128, 2, 1024], BF16)
        Wbuf = sb("Wbuf", [128, 2, 1024], BF16)
        Sbf = sb("Sbf", [128, 2, 1024], BF16)
        Mt = sb("Mt", [128, 2, 2, 512], BF16)

        ZZ = [es.enter_context(nc.psum_tensor("ZZ%d" % i, [128, 1024], F32)) for i in range(2)]
        ACC = [ps("ACC%d" % i) for i in range(2)]
        PB = [ps("PB%d" % i) for i in range(2)]

        wst = carve(16384, 24576, F32, "p (c n) -> p c n", c=8)
        wmst = carve(40960, 4096, F32, "p (c n) -> p c n", c=8)
        cst_st = carve(45056, 2560, F32, "p (c n) -> p c n", c=5)
        cmask_st = carve(47616, 512, F32)
        cmask_s_st = carve(48128, 2048, F32)
        memxs = carve(50176, 8192, F32, "p (c n) -> p c n", c=8)
        Wmt = carve(58368, 2048, BF16, "p (c n) -> p c n", c=8)
        Wmf = carve(60416, 2048, BF16, "p (c n) -> p c n", c=8)
        xs = carve(0, 16384, F32, "p (c n) -> p c n", c=8)
        kT_all = carve(16384, 17408, BF16)
        V_all = carve(33792, 16384, BF16, "p (k n) -> p k n", k=64)
        kbf = [carve(34816 + 4096 * i, 4096, BF16, "p (b n) -> p b n", b=8) for i in range(2)]
        vbf = [carve(43008 + 4096 * i, 4096, BF16, "p (l b n) -> p l b n", l=2, b=8) for i in range(2)]
        cslots = sb("cslots", [128, 4, 2048], BF16)
        kbf += [cslots[:, i, :].rearrange("p (b n) -> p b n", b=8) for i in range(2)]
        vbf += [cslots[:, 2 + i, :].rearrange("p (l b n) -> p l b n", l=2, b=8) for i in range(2)]
        kTs_bf = carve(61440, 1024, BF16)
        Vs_bf = carve(51200, 2048, BF16, "p (b n) -> p b n", b=8)
        mkc_bf = carve(53248, 4096, BF16, "p (b n) -> p b n", b=8)
        mvc_bf = carve(57344, 4096, BF16, "p (m b n) -> p m b n", m=2, b=8)
        MG = carve(0, 34816, BF16, "p (c n) -> p c n", c=8)
        gfin_t = xb[:, 0:4, :].bitcast(F32).rearrange("p c n -> p (c n)")
        xr = [kv32[:, 4 * i:4 * i + 4, :].rearrange("p c n -> p (c n)") for i in range(2)]
        y1s = [carve(34816 + 4096 * i, 4096, F32) for i in range(2)]
        yo = [carve(43008 + 4096 * i, 4096, F32) for i in range(2)]
        ssq_t = sb("ssq_t", [128, 4], F32)
        MGs = sb("MGs", [128, 8, 128], BF16)

        def mm(out, lhsT, rhs, start, stop):
            return lambda e: e.matmul(out, lhsT, rhs, start=start, stop=stop, skip_group_check=True)

        def actf(out, in_, func, scale=1.0, bias=0.0):
            return lambda e: e.activation(out=out, in_=in_, func=func, scale=scale, bias=bias)

        def tt(out, a, b, op):
            return lambda e: e.tensor_tensor(out=out, in0=a, in1=b, op=op)

        def tsm(out, a, s):
            return lambda e: e.tensor_scalar(out=out, in0=a, scalar1=s, scalar2=None, op0=ALU.mult)

        def stt(out, a, s, b, op0, op1):
            return lambda e: e.scalar_tensor_tensor(out=out, in0=a, scalar=s, in1=b, op0=op0, op1=op1)

        def cp(out, in_):
            return lambda e: e.tensor_copy(out, in_)

        def mset(ap, v):
            return lambda e: e.memset(ap, v)

        def dma(out, in_):
            return lambda e: e.dma_start(out=out, in_=in_)

        P.add("sp", dma(cst_st, cst), w=["cst_st"], dsem="d_c")
        P.add("sp", dma(cmask_st, cmask), w=["cmask_st"], dsem="d_c")
        P.add("sp", dma(cmask_s_st, cmask_s), w=["cmask_s_st"], dsem="d_c")
        P.add("sp", dma(gn_t[:], gn), w=["gn"], dsem="d_c")
        P.add("sp", dma(gm_t[:], gm), w=["gm"], dsem="d_c")
        P.add("sp", dma(selw_t[0:64, :], selw), w=["selw"], dsem="d_c")
        P.add("sp", dma(pscale_t[0:64, :], pscale), w=["pscale"], dsem="d_c")
        P.add("sp", dma(corr_t[0:64, :], corr), w=["corr"], dsem="d_c")
        tlast = P.add("sp", dma(poolw32[0:64, :], poolw), w=["poolw32"], dsem="d_c")
        P.group(["cst_st", "cmask_st", "cmask_s_st", "gn", "gm", "selw", "pscale", "corr", "poolw32"], tlast)
        P.add("sp", dma(wmst, wmsel.rearrange("(c p) n -> p c n", p=128)), w=["wmst"], dsem="d_w")
        tlast = P.add("sp", dma(memxs, memT.rearrange("(c p) n -> p c n", p=128)), w=["memxs"], dsem="d_w")
        P.group(["wmst", "memxs"], tlast)
        for hh in range(2):
            tlast = P.add("sp", dma(wst[:, 4 * hh:4 * hh + 4, :],
                                    wsel.rearrange("(c p) n -> p c n", p=128)[:, 4 * hh:4 * hh + 4, :]),
                          w=["wst%d" % hh], dsem="d_w2")
        P.group(["wst0", "wst1"], tlast)

        for i, t in enumerate([tri_bf, negones_bf, ident_bf, ones_bf, onespad_bf]):
            P.add("dve", cp(t[:], cst_st[:, i, :]), r=["cst_st"], w=["const%d" % i])
        P.add("dve", cp(cmask_bf[:], cmask_st), r=["cmask_st"], w=["cmask_bf"])
        P.add("dve", cp(cmask_s_bf[:], cmask_s_st), r=["cmask_s_st"], w=["cmask_s_bf"])
        P.add("dve", mset(one32[:], 1.0), w=["one32"])
        P.add("dve", cp(poolw_bf[0:64, :], poolw32[0:64, :]), r=["poolw32"], w=["poolw_bf"])
        WG = ["Wg%d" % c for c in range(8)]
        P.add("dve", mset(Wmf, 0.0), w=["Wmf"])
        for c in range(8):
            P.add("dve", tsm(Wmt[:, c, :], wmst[:, c, :], gm_t[:, c:c + 1]), r=["wmst", "gm"], rw=["Wmt"])
        P.add("dve", cp(Wmf[:, :, 64:128], Wmt[:, :, 0:64]), r=["Wmt"], rw=["Wmf"])
        P.add("dve", mset(mvpad_bf[:], 0.0), w=["mvpad"])
        for i in range(2):
            P.add("pool", dma(WO[:, 4 * i:4 * i + 4, :], wout.rearrange("(c p) n -> p c n", p=128)[:, 4 * i:4 * i + 4, :]),
                  rw=["WO"], dsem="d_wo")
        P.group_last(["WO"])

        pbi = [0]

        def next_pb():
            i = pbi[0]
            pbi[0] ^= 1
            return i

        def rms_stats(src, srckeys, n, psub, nsub):
            for c in range(8):
                P.add("dve", cp(xb[:, c, 0:n], src[:, c, 0:n]), r=srckeys, w=["xb%d" % c])
            b = next_pb()
            for c in range(8):
                P.add("dve", tt(xsq[:, c, 0:n], src[:, c, 0:n], src[:, c, 0:n], ALU.mult),
                      r=srckeys, w=["xsq%d" % c])
                P.add("pe", mm(PB[b][:, 0:n], ones_bf[:], xsq[:, c, 0:n], c == 0, c == 7),
                      r=["xsq%d" % c, "const3"], rw=["PB%d" % b])
            P.add("act", actf(lnms[:, 0:n], PB[b][:, 0:n], AF.Ln, scale=1.0 / D, bias=EPS),
                  r=["PB%d" % b], w=["lnms"])
            P.add("act", actf(rstd_bc[:, 0:n], lnms[:, 0:n], AF.Exp, scale=-0.5), r=["lnms"], w=["rstd_bc"])
            b = next_pb()
            for s in range(nsub):
                P.add("pe", mm(PB[b][0:psub, s:s + 1], rstd_bc[0:1, s * psub:(s + 1) * psub], one32[0:1, 0:1],
                               s == 0, s == nsub - 1),
                      r=["rstd_bc", "one32"], rw=["PB%d" % b])
            P.add("dve", cp(rstd_tm[0:psub, 0:nsub], PB[b][0:psub, 0:nsub]), r=["PB%d" % b], w=["rstd_tm"])

        XB = ["xb%d" % c for c in range(8)]

        rms_stats(memxs, ["memxs"], 256, 128, 2)
        b = next_pb()
        for c in range(8):
            P.add("pe", mm(PB[b][:, 0:256], Wmf[:, c, :], xb[:, c, 0:256], c == 0, c == 7),
                  r=["Wmf", "xb%d" % c], rw=["PB%d" % b])
        P.add("dve", tt(mkT_bf[64:128, :], PB[b][64:128, 0:256], rstd_bc[64:128, 0:256], ALU.mult),
              r=["PB%d" % b, "rstd_bc"], w=["mkT_bf"])
        b = next_pb()
        for m in range(2):
            for c in range(8):
                P.add("pe", mm(PB[b][:, m * 128:(m + 1) * 128], xb[:, c, m * 128:(m + 1) * 128], Wmt[:, c, :],
                               c == 0, c == 7),
                      r=["Wmt", "xb%d" % c], rw=["PB%d" % b])
        for m in range(2):
            P.add("dve", tsm(kv32[:, m, 0:128], PB[b][:, m * 128:(m + 1) * 128], rstd_tm[:, m:m + 1]),
                  r=["PB%d" % b, "rstd_tm"], rw=["kv32"])
        P.add("dve", cp(mvpad_bf[:, :, 64:128], kv32[:, 0:2, 64:128]), r=["kv32"], rw=["mvpad"])
        P.add("pool", dma(memkv_out.rearrange("(m p) f -> p m f", p=128), kv32[:, 0:2, 0:128]),
              r=["kv32"], dsem="o_misc", final=True)

        for c in range(8):
            P.add("dve", tsm(Wg[:, c, :], wst[:, c, :], gn_t[:, c:c + 1]),
                  r=["wst%d" % (c // 4), "gn"], w=["Wg%d" % c])

        state = {"blk": 0}
        PROMPT_ARENA_KEYS = ["kT%d" % i for i in range(NT)] + ["V%d" % i for i in range(NT)]

        SETUP_ARENA_KEYS = ["Wmt", "Wmf", "memxs"]

        def prefetch_chunk(ch):
            if ch < 0:
                return
            slot = ch % 4
            ov = PROMPT_ARENA_KEYS if ch in (13, 12) else []
            P.add("pool", dma(kbf[slot], kcT[ch].rearrange("p (b n) -> p b n", b=8)),
                  w=["kbf%d" % slot] + ov, dsem="d_kv%d" % slot)
            P.add("pool", dma(vbf[slot], vc[ch].rearrange("p (l b n) -> p l b n", l=2, b=8)),
                  w=["vbf%d" % slot] + ov, dsem="d_kv%d" % slot)
            P.group_last(["kbf%d" % slot, "vbf%d" % slot])

        def load_x(ti):
            t0 = ti * TW
            for hh in range(2):
                P.add("sp", dma(xs[:, 4 * hh:4 * hh + 4, :],
                                xT.rearrange("(c p) t -> p c t", p=128)[:, 4 * hh:4 * hh + 4, t0:t0 + TW]),
                      w=["xs%d" % hh], dsem="d_x%d" % hh)

        def tile_front(ti, nxt):
            sample = (ti == NT)
            t0 = ti * TW
            par = ti % 2
            qT = qT2[:, par, :]
            sg1 = sg1_2[:, par, :]
            qTk, sg1k = "qT%d" % par, "sg1_%d" % par
            psub, nsub = (64, 8) if sample else (128, 4)
            nb, L = (8, 64) if sample else (1, 512)
            Wd = 16 + L
            if sample:
                P.add("pool", dma(mkc_bf[64:128, :, :], mkcT), w=["mkc_bf"] + SETUP_ARENA_KEYS, dsem="d_h2")
                P.add("pool", dma(mvc_bf, mvc.rearrange("(m p) b f -> p m b f", p=128)), w=["mvc_bf"] + SETUP_ARENA_KEYS,
                      dsem="d_h2")
                P.group_last(["mkc_bf", "mvc_bf"])
                if "nocache" not in DEBUG_S:
                    prefetch_chunk(15)
                    prefetch_chunk(14)
            rms_stats(xs, ["xs0", "xs1"], TW, psub, nsub)
            if nxt is not None:
                load_x(nxt)

            def fm_proj(col0):
                b = next_pb()
                for c in range(8):
                    P.add("pe", mm(PB[b][:, :], Wg[:, c, col0:col0 + 128], xb[:, c, :], c == 0, c == 7),
                          r=[WG[c], XB[c]], rw=["PB%d" % b])
                return b

            def do_T0():
                b = fm_proj(0)
                P.add("dve", stt(qT, PB[b][:, :], 0.125, rstd_bc[:], ALU.mult, ALU.mult),
                      r=["PB%d" % b, "rstd_bc"], w=[qTk])

            def do_T1():
                b = fm_proj(128)
                kdst = kTs_bf if sample else kT_all[:, t0:t0 + TW]
                P.add("dve", tt(kdst, PB[b][:, :], rstd_bc[:], ALU.mult),
                      r=["PB%d" % b, "rstd_bc"], w=["kT%d" % ti] + (SETUP_ARENA_KEYS if sample else []))

            def do_T3():
                b = fm_proj(384)
                P.add("dve", tt(sg1, PB[b][:, :], rstd_bc[:], ALU.mult), r=["PB%d" % b, "rstd_bc"], w=[sg1k])

            def do_T4():
                b = fm_proj(512)
                if sample:
                    P.add("dve", mset(ubuf[0:64, :], 0.0), w=["ubuf"])
                    ub3 = ubuf[0:64, :].rearrange("p (b k) -> p b k", b=8)
                    P.add("sp", dma(ub3[:, :, 1:16], hist), rw=["ubuf"], dsem="d_h")
                    P.add("dve", tt(ub3[:, :, 16:80], PB[b][0:64, :].rearrange("p (b k) -> p b k", b=8),
                                    rstd_bc[0:64, :].rearrange("p (b k) -> p b k", b=8), ALU.mult),
                          r=["PB%d" % b, "rstd_bc"], rw=["ubuf"])
                else:
                    if ti == 0:
                        P.add("dve", mset(ubuf[0:64, 0:16], 0.0), rw=["ubuf"])
                    P.add("dve", tt(ubuf[0:64, 16:528], PB[b][0:64, :], rstd_bc[0:64, :], ALU.mult),
                          r=["PB%d" % b, "rstd_bc"], rw=["ubuf"])
                P.add("dve", stt(qx[64:128, :], PB[b][64:128, :], 0.125, rstd_bc[64:128, :], ALU.mult, ALU.mult),
                      r=["PB%d" % b, "rstd_bc"], w=["qx"])

            def do_T5():
                b = fm_proj(640)
                P.add("dve", tt(sg2[:], PB[b][:, :], rstd_bc[:], ALU.mult), r=["PB%d" % b, "rstd_bc"], w=["sg2"])


            def do_TM():
                for s in range(nsub):
                    if s % 2 == 0:
                        b = next_pb()
                    co = (s % 2) * 256
                    for c in range(8):
                        P.add("pe", mm(PB[b][0:psub, co:co + 256], xb[:, c, s * psub:(s + 1) * psub], Wg[:, c, 128:384],
                                       c == 0, c == 7),
                              r=[WG[c], XB[c]], rw=["PB%d" % b])
                    P.add("dve", tsm(kv32[0:psub, s, :], PB[b][0:psub, co:co + 256], rstd_tm[0:psub, s:s + 1]),
                          r=["PB%d" % b, "rstd_tm"], rw=["kv32"])
                if sample:
                    P.add("dve", cp(Vs_bf[0:64, :, :], kv32[0:64, :, 128:256]), r=["kv32"], w=["Vs32"] + SETUP_ARENA_KEYS)
                    P.add("pool", dma(k_out[t0:t0 + TW, :].rearrange("(s p) f -> p s f", p=64), kv32[0:64, :, 0:128]),
                          r=["kv32"], dsem="o_kv", final=True)
                    P.add("pool", dma(v_out[t0:t0 + TW, :].rearrange("(s p) f -> p s f", p=64), kv32[0:64, :, 128:256]),
                          r=["kv32"], dsem="o_kv", final=True)
                else:
                    P.add("dve", cp(V_all[:, 4 * ti:4 * ti + 4, :], kv32[:, 0:4, 128:256]),
                          r=["kv32"], w=["V%d" % ti])
                    P.add("pool", dma(k_out[t0:t0 + TW, :].rearrange("(s p) f -> p s f", p=128), kv32[:, 0:4, 0:128]),
                          r=["kv32"], dsem="o_kv", final=True)
                    P.add("pool", dma(v_out[t0:t0 + TW, :].rearrange("(s p) f -> p s f", p=128), kv32[:, 0:4, 128:256]),
                          r=["kv32"], dsem="o_kv", final=True)


            def silu_gate(g, tmp, key, tkey):
                P.add("act", actf(tmp[:], g, AF.Exp, scale=-1.0), r=[key], w=[tkey])
                P.add("act", actf(tmp[:], tmp[:], AF.Ln, bias=1.0), rw=[tkey])
                P.add("act", actf(tmp[:], tmp[:], AF.Exp, scale=-1.0), rw=[tkey])
                P.add("dve", tt(g, g, tmp[:], ALU.mult), r=[tkey], rw=[key])

            if sample:
                U = ubuf[0:64, :].rearrange("p (b k) -> p b k", b=8)
                A = sa[0:64, :].rearrange("p (b k) -> p b k", b=8)
                B_ = sbb[0:64, :].rearrange("p (b k) -> p b k", b=8)
                PA = pacc[0:64, :].rearrange("p (b k) -> p b k", b=8)
                PO = pooled[0:64, :].rearrange("p (b k) -> p b k", b=8)
                sl = lambda X, a, bb_: X[:, :, a:bb_]
            else:
                U = ubuf[0:64, 0:528]
                A = sa[0:64, 0:528]
                B_ = sbb[0:64, 0:528]
                PA = pacc[0:64, :]
                PO = pooled[0:64, :]
                sl = lambda X, a, bb_: X[:, a:bb_]
            full = lambda X: X[:, :, :] if sample else X[:, :]
            def pool_part0():
                P.add("dve", tt(sl(A, 1, Wd), sl(U, 1, Wd), sl(U, 0, Wd - 1), ALU.add), r=["ubuf"], w=["sa"])
                P.add("dve", tsm(full(PA), sl(A, 16, Wd), selw_t[0:64, 0:1]), r=["sa", "selw"], w=["pacc"])

            def pool_part1():
                P.add("dve", tt(sl(B_, 3, Wd), sl(A, 3, Wd), sl(A, 1, Wd - 2), ALU.add), r=["sa"], w=["sbb"])
                P.add("dve", stt(full(PA), sl(B_, 16, Wd), selw_t[0:64, 1:2], full(PA), ALU.mult, ALU.add),
                      r=["sbb", "selw"], rw=["pacc"])

            def pool_part2():
                P.add("dve", tt(sl(A, 7, Wd), sl(B_, 7, Wd), sl(B_, 3, Wd - 4), ALU.add), r=["sbb"], w=["sa"])
                P.add("dve", stt(full(PA), sl(A, 16, Wd), selw_t[0:64, 2:3], full(PA), ALU.mult, ALU.add),
                      r=["sa", "selw"], rw=["pacc"])

            def pool_part3():
                P.add("dve", tt(sl(B_, 15, Wd), sl(A, 15, Wd), sl(A, 7, Wd - 8), ALU.add), r=["sa"], w=["sbb"])
                P.add("dve", stt(full(PA), sl(B_, 16, Wd), selw_t[0:64, 3:4], full(PA), ALU.mult, ALU.add),
                      r=["sbb", "selw"], rw=["pacc"])
                if ti == 0:
                    P.add("dve", tt(pacc[0:64, :], pacc[0:64, :], corr_t[0:64, :], ALU.mult), r=["corr"], rw=["pacc"])
                P.add("dve", tt(full(PO), full(PA), sl(U, 16, Wd), ALU.subtract), r=["pacc", "ubuf"], w=["pooled"])
                if sample:
                    P.add("pool", dma(u_out[:, 15:135].rearrange("p (b k) -> p b k", b=8), U[:, :, 65:80]),
                          r=["ubuf"], dsem="o_misc", final=True)
                else:
                    if ti == NT - 1:
                        P.add("pool", dma(u_out[:, 0:15], ubuf[0:64, 513:528]), r=["ubuf"], dsem="o_misc", final=True)
                    P.add("dve", cp(ubuf[0:64, 0:16], ubuf[0:64, 512:528]), rw=["ubuf"])

            def pool_mm():
                b = next_pb()
                P.add("pe", mm(PB[b][0:64, :], poolw_bf[0:64, :], pooled[0:64, :], True, True),
                      r=["poolw_bf", "pooled"], rw=["PB%d" % b])
                P.add("dve", tsm(O2[0:64, :], PB[b][0:64, :], pscale_t[0:64, 0:1]), r=["PB%d" % b, "pscale"], rw=["O2"])


            def do_XA():
                bs = [next_pb(), next_pb()]
                for m in range(2):
                    if sample:
                        for bb in range(8):
                            P.add("pe", mm(PB[bs[m]][:, bb * 64:(bb + 1) * 64], mkc_bf[64:128, bb, m * 128:(m + 1) * 128],
                                           qx[64:128, bb * 64:(bb + 1) * 64], bb == 0, bb == 7),
                                  r=["mkc_bf", "qx"], rw=["PB%d" % bs[m]])
                    else:
                        P.add("pe", mm(PB[bs[m]][:, :], mkT_bf[64:128, m * 128:(m + 1) * 128], qx[64:128, :], True, True),
                              r=["mkT_bf", "qx"], rw=["PB%d" % bs[m]])
                    P.add("act", actf(pT[:, m, :], PB[bs[m]][:, :], AF.Exp), r=["PB%d" % bs[m]], w=["pT%d" % m])
                bo, bd = bs
                for m in range(2):
                    if sample:
                        for bb in range(8):
                            P.add("pe", mm(PB[bo][:, bb * 64:(bb + 1) * 64], mvc_bf[:, m, bb, :],
                                           pT[:, m, bb * 64:(bb + 1) * 64], (m == 0 and bb == 0), (m == 1 and bb == 7)),
                                  r=["mvc_bf", "pT%d" % m], rw=["PB%d" % bo])
                    else:
                        P.add("pe", mm(PB[bo][:, :], mvpad_bf[:, m, :], pT[:, m, :], m == 0, m == 1),
                              r=["mvpad", "pT%d" % m], rw=["PB%d" % bo])
                for m in range(2):
                    P.add("pe", mm(PB[bd][:, :], onespad_bf[:], pT[:, m, :], m == 0, m == 1),
                          r=["const4", "pT%d" % m], rw=["PB%d" % bd])
                P.add("act", actf(rden[64:128, :], PB[bd][64:128, :], AF.Ln), r=["PB%d" % bd], w=["rden"])
                P.add("act", actf(rden[64:128, :], rden[64:128, :], AF.Exp, scale=-1.0), rw=["rden"])
                P.add("dve", tt(O2[64:128, :], PB[bo][64:128, :], rden[64:128, :], ALU.mult),
                      r=["PB%d" % bo, "rden"], rw=["O2"])


            do_T4()
            do_T5()
            pool_part0()
            do_T3()
            silu_gate(sg2[:], tmpb, "sg2", "tmpb")
            pool_part1()
            do_T0()
            pool_part2()
            silu_gate(sg1, tmpa, sg1k, "tmpa")
            do_T1()
            pool_part3()
            do_TM()
            pool_mm()
            do_XA()
            P.add("dve", tt(Mt[:, par, 1, :], O2[:], sg2[:], ALU.mult), r=["O2", "sg2"], rw=["Mt%d" % par])

        def tile_attn(ti, bg):
            sample = (ti == NT)
            par = ti % 2
            qT = qT2[:, par, :]
            sg1 = sg1_2[:, par, :]
            qTk, sg1k = "qT%d" % par, "sg1_%d" % par
            blocks = []
            if sample:
                if "nonew" not in DEBUG_S:
                    blocks.append(dict(kind="new", c0=0, kp=64))
                if "nocache" not in DEBUG_S:
                    for kb in range(31, -1, -1):
                        blocks.append(dict(kind="cache", kb=kb, c0=0, kp=128))
            else:
                for kb in range(4 * ti + 3, -1, -1):
                    c0 = 128 * (kb - 4 * ti) if kb >= 4 * ti else 0
                    blocks.append(dict(kind="prompt", kb=kb, c0=c0, kp=128, diag=(kb >= 4 * ti)))
            NB = len(blocks)
            for i, bl in enumerate(blocks):
                bl["first"] = (i == 0)
                bl["last"] = (i == NB - 1)
            P.add("dve", mset(Sbf[:].bitcast(F32), 0.0), w=["Sbf0", "Sbf1"])
            gbase = state["blk"]
            state["blk"] += NB

            def zi(n):
                return (gbase + n) % 2

            def zkeys(n):
                return ["ZZ%d_0" % zi(n), "ZZ%d_1" % zi(n)]

            def ZZv(n):
                return ZZ[zi(n)].rearrange("p (h f) -> p h f", h=2)

            def sl2(n):
                return (gbase + n) % 2

            Ev = Ebuf.rearrange("p s (h f) -> p s h f", h=2)
            Lv = Lp.rearrange("p s (h f) -> p s h f", h=2)
            Wv = Wbuf.rearrange("p s (h f) -> p s h f", h=2)
            Sv = Sbf.rearrange("p s (h f) -> p s h f", h=2)

            if sample and "nocache" not in DEBUG_S:
                for ch0 in (13, 12):
                    prefetch_chunk(ch0)

            def addZ(n):
                bl = blocks[n]
                c0, kp = bl["c0"], bl["kp"]
                zz = ZZ[zi(n)]
                for h in range(2):
                    zk = ["ZZ%d_%d" % (zi(n), h)]
                    o = h * 512
                    if bl["kind"] == "prompt":
                        kb = bl["kb"]
                        P.add("pe", mm(zz[:, o + c0:o + 512], kT_all[64 * h:64 * h + 64, kb * 128:(kb + 1) * 128],
                                       qT[64 * h:64 * h + 64, c0:512], True, True),
                              r=["kT%d" % (kb // 4), qTk], w=zk)
                        if bl["diag"]:
                            P.add("pe", mm(zz[:, o + c0:o + c0 + 128], ident_bf[:], cmask_bf[:], False, True),
                                  r=["const2", "cmask_bf"], rw=zk)
                    elif bl["kind"] == "new":
                        for bb in range(8):
                            P.add("pe", mm(zz[0:64, o + bb * 64:o + (bb + 1) * 64],
                                           kTs_bf[64 * h:64 * h + 64, bb * 64:(bb + 1) * 64],
                                           qT[64 * h:64 * h + 64, bb * 64:(bb + 1) * 64], bb == 0, True),
                                  r=["kT%d" % NT, qTk], rw=zk)
                        P.add("pe", mm(zz[0:64, o:o + 512], ident_bf[0:64, 0:64], cmask_s_bf[0:64, :], False, True),
                              r=["const2", "cmask_s_bf"], rw=zk)
                    else:
                        kb = bl["kb"]
                        slot, loc = (kb // 2) % 4, kb % 2
                        for bb in range(8):
                            P.add("pe", mm(zz[:, o + bb * 64:o + (bb + 1) * 64],
                                           kbf[slot][64 * h:64 * h + 64, bb, loc * 128:(loc + 1) * 128],
                                           qT[64 * h:64 * h + 64, bb * 64:(bb + 1) * 64], bb == 0, True),
                                  r=["kbf%d" % slot, qTk], rw=zk)

            def addP1(n):
                bl = blocks[n]
                c0, kp = bl["c0"], bl["kp"]
                P.add("act", actf(Ev[0:kp, sl2(n), :, c0:512], ZZv(n)[0:kp, :, c0:512], AF.Exp),
                      r=zkeys(n), w=["E%d" % sl2(n)])

            def addP2(n):
                bl = blocks[n]
                c0, kp = bl["c0"], bl["kp"]
                P.add("act", actf(Lv[0:kp, sl2(n), :, c0:512], Ev[0:kp, sl2(n), :, c0:512], AF.Ln, bias=1.0),
                      r=["E%d" % sl2(n)], w=["Lp%d" % sl2(n)])

            def addSA(n):
                bl = blocks[n]
                if bl["last"]:
                    return
                c0, kp = bl["c0"], bl["kp"]
                cur, nx = sl2(n), sl2(n + 1)
                P.add("dve", tt(Sv[0:kp, nx, :, c0:512], Sv[0:kp, cur, :, c0:512], Lv[0:kp, sl2(n), :, c0:512], ALU.add),
                      r=["Lp%d" % sl2(n), "Sbf%d" % cur], rw=["Sbf%d" % nx])

            def addNO(n):
                bl = blocks[n]
                if bl["first"]:
                    return
                c0, kp = bl["c0"], bl["kp"]
                zz = ZZ[zi(n)]
                ss = sl2(n)
                for h in range(2):
                    o = h * 512
                    P.add("pe", mm(zz[0:kp, o + c0:o + 512], negones_bf[:, 0:kp], Sv[:, ss, h, c0:512], False, False),
                          r=["const1", "Sbf%d" % ss], rw=["ZZ%d_%d" % (zi(n), h)])

            def addTRI(n):
                bl = blocks[n]
                c0, kp = bl["c0"], bl["kp"]
                zz = ZZ[zi(n)]
                for h in range(2):
                    o = h * 512
                    P.add("pe", mm(zz[0:kp, o + c0:o + 512], tri_bf[0:kp, 0:kp], Lv[0:kp, sl2(n), h, c0:512], False, True),
                          r=["const0", "Lp%d" % sl2(n)], rw=["ZZ%d_%d" % (zi(n), h)])

            def addP3(n):
                bl = blocks[n]
                c0, kp = bl["c0"], bl["kp"]
                P.add("act", actf(Wv[0:kp, sl2(n), :, c0:512], ZZv(n)[0:kp, :, c0:512], AF.Exp),
                      r=zkeys(n), w=["W%d" % sl2(n)])

            def addAV(n):
                bl = blocks[n]
                c0, kp = bl["c0"], bl["kp"]
                for h in range(2):
                    acc = ACC[h]
                    ak = "ACC%d" % h
                    if bl["kind"] == "prompt":
                        kb = bl["kb"]
                        P.add("pe", mm(acc[:, c0:512], V_all[:, kb, :], Wv[:, sl2(n), h, c0:512], bl["first"], bl["last"]),
                              r=["V%d" % (kb // 4), "W%d" % sl2(n)], rw=[ak])
                    elif bl["kind"] == "new":
                        for bb in range(8):
                            P.add("pe", mm(acc[:, bb * 64:(bb + 1) * 64], Vs_bf[0:64, bb, :],
                                           Wv[0:64, sl2(n), h, bb * 64:(bb + 1) * 64], bb == 0, False),
                                  r=["Vs32", "W%d" % sl2(n)], rw=[ak])
                    else:
                        kb = bl["kb"]
                        slot, loc = (kb // 2) % 4, kb % 2
                        for bb in range(8):
                            P.add("pe", mm(acc[:, bb * 64:(bb + 1) * 64], vbf[slot][:, loc, bb, :],
                                           Wv[:, sl2(n), h, bb * 64:(bb + 1) * 64], False, bl["last"]),
                                  r=["vbf%d" % slot, "W%d" % sl2(n)], rw=[ak])
                if bl["kind"] == "cache" and bl["kb"] % 2 == 0:
                    prefetch_chunk(bl["kb"] // 2 - 4)

            nodes = P.plan_bg(bg)
            NBLK_LONG[0] = NB >= 28
            cnt = {}
            for nd in nodes:
                cnt[nd["eng"]] = cnt.get(nd["eng"], 0) + 1
            base = {"pe": 5, "dve": 2, "act": 1, "pool": 2, "sp": 2}
            nit = max(NB - 2, 1)
            caps = {e: max(base[e], -(-cnt.get(e, 0) // nit)) for e in base}
            if NB > 0:
                addZ(0)
            for n in range(0, NB + 1):
                if n < NB:
                    addP1(n)
                    addNO(n)
                if n - 1 >= 0:
                    addP3(n - 1)
                if n + 1 < NB:
                    addZ(n + 1)
                if n < NB:
                    addP2(n)
                    addSA(n)
                    addTRI(n)
                if n - 1 >= 0:
                    addAV(n - 1)
                if n < NB:
                    P.issue_bg(nodes, n, caps)
                if sample and n == 14:
                    emit_prompt_gathers()
            P.issue_bg(nodes, NB + 1, caps, flush=True)

            slot = par
            P.add("dve", tt(Mt[0:64, slot, 0, :], ACC[0][0:64, :], sg1[0:64, :], ALU.mult),
                  r=["ACC0", sg1k], rw=["Mt%d" % slot])
            P.add("dve", tt(Mt[64:128, slot, 0, :], ACC[1][64:128, :], sg1[64:128, :], ALU.mult),
                  r=["ACC1", sg1k], rw=["Mt%d" % slot])
            if DEBUG_DUMP:
                P.add("dve", cp(tmpa[0:64, :], ACC[0][0:64, :]), r=["ACC0"], rw=["tmpa"])
                P.add("dve", cp(tmpa[64:128, :], ACC[1][64:128, :]), r=["ACC1"], rw=["tmpa"])
                P.add("pool", dma(dbg[ti, 0], tmpa[:]), r=["tmpa"], dsem="o_dbg", final=True)
                P.add("pool", dma(dbg[ti, 1], O2[:]), r=["O2"], dsem="o_dbg", final=True)
                P.add("pool", dma(dbg[ti, 2], sg1), r=[sg1k], dsem="o_dbg", final=True)
                P.add("pool", dma(dbg[ti, 3], sg2[:]), r=["sg2"], dsem="o_dbg", final=True)
                P.add("pool", dma(dbgM[ti].rearrange("t p f -> p t f"), Mt[:, slot, :, :]), r=["Mt%d" % slot], dsem="o_dbg", final=True)
            if sample:
                ccs = cc_in_s.rearrange("(j t p) f -> j t p f", j=4, t=2)
                for t in range(2):
                    P.add("pool", dma(ccs[:, t, :, :].rearrange("j p f -> p j f"),
                                      Mt[:, slot, t, :].rearrange("p (j f) -> p j f", j=4)),
                          r=["Mt%d" % slot], rw=["cc_in_s"], dsem="o_m%d" % slot)
                P.add("pool", lambda e: e.collective_compute("AllGather", ALU.bypass, replica_groups=RG,
                                                             ins=[cc_in_s], outs=[cc_out_s]),
                      r=["cc_in_s"], w=["cc_out_s"], dsem="d_cc", inc=1)
            else:
                g, cc0 = ti // 4, (ti % 4) * 512
                ccv = cc_in.rearrange("(g t p) f -> g t p f", g=4, t=2)
                P.add("pool", dma(ccv[g, :, :, cc0:cc0 + 512].rearrange("t p f -> p t f"), Mt[:, slot, :, :]),
                      r=["Mt%d" % slot], rw=["cc_in%d" % g], dsem="o_m%d" % slot)
                if ti % 4 == 3 and not DEBUG_NO_P4:
                    P.add("pool", (lambda g_: (lambda e: e.collective_compute(
                        "AllGather", ALU.bypass, replica_groups=RG,
                        ins=[cc_in[g_ * 256:(g_ + 1) * 256, :]], outs=[cc_out[g_ * 1024:(g_ + 1) * 1024, :]])))(g),
                          r=["cc_in%d" % g], w=["cc_out%d" % g], dsem="d_cc", inc=1)

        gath = {}

        def gather_dma(ch, smp):
            def fn(e):
                if "my" not in gath:
                    gath["my"] = e.partition_id() % 4
                if smp:
                    view = cc_out_s.rearrange("(r j t p) f -> j p r t f", r=4, j=4, t=2)
                    src = view[bass.ds(gath["my"], 1)][0]
                    return e.dma_start(out=MGs[:, ch, :], in_=src[:, ch // 2, ch % 2, :])
                view = cc_out.rearrange("(g r t p) f -> g p r t f", g=4, r=4, t=2)
                src = view[bass.ds(gath["my"], 1)][0]
                return e.dma_start(out=MG[:, ch, 0:2048], in_=src[:, ch // 2, ch % 2, :])
            return fn

        gstate = {"done": False}

        def emit_prompt_gathers():
            if gstate["done"] or DEBUG_NO_P4:
                return
            gstate["done"] = True
            mg_reads = ["kT%d" % i for i in range(NT)] + ["xs0", "xs1"]
            cco = ["cc_out%d" % g for g in range(4)]
            tl = None
            for ch in range(8):
                tl = P.add("pool", gather_dma(ch, False), r=cco, w=["MGc%d" % ch] + (mg_reads + ["MG0", "MG1"] if ch == 0 else []),
                           dsem="d_g")
            P.group(["MG0", "MG1"], tl)

        P4_SCRATCH = ["kbf0", "kbf1", "vbf0", "vbf1"] + ["V%d" % i for i in range(NT)]

        ntiles = NT if DEBUG_TILES is None else DEBUG_TILES
        tiles = list(range(ntiles)) + ([] if DEBUG_NO_SAMPLE else [NT])
        if tiles:
            load_x(tiles[0])
            tile_front(tiles[0], tiles[1] if len(tiles) > 1 else None)
        for i, ti in enumerate(tiles):
            bg = []
            if i + 1 < len(tiles):
                P.begin_defer()
                tile_front(tiles[i + 1], tiles[i + 2] if i + 2 < len(tiles) else None)
                bg = P.end_defer()
            tile_attn(ti, bg)

        if not DEBUG_NO_P4:
            emit_prompt_gathers()

            def sample_gathers():
                tl = None
                for ch in range(8):
                    tl = P.add("pool", gather_dma(ch, True), r=["cc_out_s"], w=["MGsc%d" % ch] + (["MGs"] if ch == 0 else []),
                               dsem="d_gs")
                P.group(["MGs"], tl)
            if DEBUG_DUMP:
                P.add("pool", dma(dbgMG, MG), r=["MG0", "MG1"], dsem="o_dbg", final=True)
            P.add("sp", dma(gfin_t, gfin), w=XB[0:4], dsem="d_h")

            nst = OUTTOK // 128
            for st in range(nst):
                slot = st % 2
                smp_st = (st == nst - 1)
                if smp_st and not DEBUG_NO_SAMPLE:
                    sample_gathers()
                P.add("sp", dma(xr[slot], xres[st * 128:(st + 1) * 128, :]),
                      w=["xr%d" % slot] + (["kv32"] if st < 2 else []), dsem="d_x%d" % slot)
                y1 = y1s[slot]
                y1k = "y1_%d" % slot
                for nh in range(2):
                    zb = ZZ[st % 2][:, nh * 512:(nh + 1) * 512]
                    zk = "ZZ%d_%d" % (st % 2, nh)
                    for ch in range(8):
                        lhs = MGs[:, ch, :] if smp_st else MG[:, ch, st * 128:(st + 1) * 128]
                        P.add("pe", mm(zb, lhs, WO[:, ch, nh * 512:(nh + 1) * 512], ch == 0, ch == 7),
                              r=[("MGs" if smp_st else "MG%d" % (ch // 4)), "WO"], rw=[zk])
                    P.add("dve", tt(y1[:, nh * 512:(nh + 1) * 512], zb, xr[slot][:, nh * 512:(nh + 1) * 512], ALU.add),
                          r=[zk, "xr%d" % slot], rw=[y1k], w=(P4_SCRATCH if st < 2 else []))
                if DEBUG_DUMP:
                    P.add("pool", dma(dbgY1[st * 128:(st + 1) * 128, :], y1), r=[y1k], dsem="o_dbg", final=True)
                P.add("act", actf(yo[slot], y1, AF.Square), r=[y1k], w=["yo%d" % slot] + (P4_SCRATCH if st < 2 else []))
                P.add("dve", lambda e, sl_=slot: e.reduce_sum(out=ssq_t[:, 0:1], in_=yo[sl_], axis=mybir.AxisListType.X),
                      r=["yo%d" % slot], w=["ssq"])
                P.add("act", actf(ssq_t[:, 1:2], ssq_t[:, 0:1], AF.Ln, scale=1.0 / D, bias=EPS), r=["ssq"], w=["ssq1"])
                P.add("act", actf(ssq_t[:, 2:3], ssq_t[:, 1:2], AF.Exp, scale=-0.5), r=["ssq1"], w=["ssq2"])
                P.add("dve", stt(yo[slot], y1, ssq_t[:, 2:3], gfin_t, ALU.mult, ALU.mult),
                      r=[y1k, "ssq2"] + XB[0:4], rw=["yo%d" % slot])
                P.add("act", dma(y_out[st * 128:(st + 1) * 128, :], yo[slot]), r=["yo%d" % slot],
                      dsem="o_y%d" % slot, final=True)


        semnames = P.finalize()
        sems = {n: es.enter_context(nc.semaphore(n)) for n in semnames}
        with nc.Block() as block:
            @block.sync
            def _(e):
                P.emit("sp", e, sems)

            @block.tensor
            def _(e):
                P.emit("pe", e, sems)

            @block.scalar
            def _(e):
                P.emit("act", e, sems)

            @block.vector
            def _(e):
                P.emit("dve", e, sems)

            @block.gpsimd
            def _(e):
                P.emit("pool", e, sems, final_waits=True)
    return nc


_NC_CACHE = {}


def _consts():
    k = np.arange(128)[:, None]
    m = np.arange(128)[None, :]
    tri = np.where(k >= m, -1.0, 0.0).astype(np.float32)
    negones = -np.ones((128, 128), np.float32)
    ident = np.eye(128, dtype=np.float32)
    ones = np.ones((128, 128), np.float32)
    onespad = np.concatenate([np.zeros((128, 64), np.float32), np.ones((128, 64), np.float32)], axis=1)
    cst = np.stack([tri, negones, ident, ones, onespad], axis=1)
    cmask = np.where(k >= m, NEG, 0.0).astype(np.float32)
    k64 = np.arange(128)[:, None]
    f = np.arange(512)[None, :] % 64
    cmask_s = np.where(k64 >= f, NEG, 0.0).astype(np.float32)
    return np.ascontiguousarray(cst), cmask, np.ascontiguousarray(cmask_s)


def _prep_inputs(c, x_prompt, x_sample, cache_sb_k, cache_sb_v, state_pool, cache_mem_k, cache_mem_v, mem_prompt,
                 g_norm, w_in, pool_w, pool_scale, g_mem, w_mem_kv, w_out, g_final):
    b, j = c // 4, c % 4
    sb_ = slice(8 * b, 8 * b + 8)
    f32 = lambda a: np.ascontiguousarray(a, dtype=np.float32)
    xs_s = x_sample[sb_].reshape(512, D)
    xT = np.concatenate([x_prompt[b].T, xs_s.T], axis=1)
    cols = np.concatenate([
        np.arange(128 * j, 128 * j + 128),
        512 + np.arange(128 * j, 128 * j + 128),
        1024 + np.arange(128 * j, 128 * j + 128),
        1536 + np.arange(128 * j, 128 * j + 128),
        2048 + np.arange(64 * j, 64 * j + 64),
        2560 + np.arange(64 * j, 64 * j + 64),
        2304 + np.arange(64 * j, 64 * j + 64),
        2816 + np.arange(64 * j, 64 * j + 64)])
    wsel = w_in[0][:, cols]
    wmsel = w_mem_kv[0][:, np.concatenate([np.arange(64 * j, 64 * j + 64), 256 + np.arange(64 * j, 64 * j + 64)])]
    wins = np.array([2, 4, 8, 16], np.float32)
    selw = np.zeros((64, 4), np.float32)
    selw[:, j] = 1.0 / wins[j]
    t = np.arange(512, dtype=np.float32)
    corr = np.broadcast_to(wins[j] / np.minimum(t + 1.0, wins[j]), (64, 512))
    rows = []
    for r in range(4):
        rows.append(np.arange(128 * r, 128 * r + 128))
        rows.append(np.concatenate([512 + np.arange(64 * r, 64 * r + 64), 768 + np.arange(64 * r, 64 * r + 64)]))
    wout = w_out[0][np.concatenate(rows), :]
    xres = np.concatenate([x_prompt[b, 2048 * j:2048 * j + 2048],
                           x_sample[8 * b + 2 * j:8 * b + 2 * j + 2].reshape(128, D)], axis=0)
    kc = cache_sb_k[0, sb_, :, 2 * j:2 * j + 2, :]
    kcT = kc.reshape(8, 16, 256, 2, 64).transpose(1, 3, 4, 0, 2).reshape(16, 128, 8 * 256)
    vcs = cache_sb_v[0, sb_, :, 2 * j:2 * j + 2, :]
    vcl = vcs.reshape(8, 16, 2, 128, 128).transpose(1, 3, 2, 0, 4).reshape(16, 128, 2 * 8 * 128)
    histl = state_pool[0, sb_, :, 64 * j:64 * j + 64].transpose(2, 0, 1)
    mkcT = cache_mem_k[0, sb_, :, j, :].transpose(2, 0, 1)
    mvc = np.zeros((256, 8, 128), np.float32)
    mvc[:, :, 64:128] = cache_mem_v[0, sb_, :, j, :].transpose(1, 0, 2)
    cst, cmask, cmask_s = _consts()
    return {
        "xT": f32(xT), "wsel": f32(wsel), "wmsel": f32(wmsel), "memT": f32(mem_prompt[b].T),
        "gn": f32(g_norm[0].reshape(8, 128).T), "gm": f32(g_mem[0].reshape(8, 128).T),
        "poolw": f32(pool_w[0, j]), "pscale": f32(pool_scale[0, 64 * j:64 * j + 64].reshape(64, 1)),
        "selw": selw, "corr": f32(corr), "wout": f32(wout),
        "gfin": f32(np.broadcast_to(g_final[None, :], (128, D))), "xres": f32(xres),
        "kcT": f32(kcT), "vc": f32(vcl), "hist": f32(histl), "mkcT": f32(mkcT), "mvc": mvc,
        "cst": cst, "cmask": cmask, "cmask_s": cmask_s,
    }


def kernel(x_prompt, x_sample, cache_sb_k, cache_sb_v, state_pool, cache_mem_k, cache_mem_v, mem_prompt,
           g_norm, w_in, pool_w, pool_scale, g_mem, w_mem_kv, w_out, g_final):
    args = [np.asarray(a) for a in (x_prompt, x_sample, cache_sb_k, cache_sb_v, state_pool, cache_mem_k,
                                    cache_mem_v, mem_prompt, g_norm, w_in, pool_w, pool_scale, g_mem,
                                    w_mem_kv, w_out, g_final)]
    if "nc" not in _NC_CACHE:
        _NC_CACHE["nc"] = build_nc()
    nc = _NC_CACHE["nc"]
    in_maps = [_prep_inputs(c, *args) for c in range(NCORES)]
    res = run_bass_kernel_spmd(nc, in_maps, core_ids=list(range(NCORES)))
    R = res.results

    y_prompt = np.zeros((2, SEQ, D), np.float32)
    y_sample = np.zeros((16, 64, D), np.float32)
    sb_k_p = np.zeros((1, 2, SEQ, 8, 64), np.float32)
    sb_v_p = np.zeros((1, 2, SEQ, 8, 64), np.float32)
    pool_p = np.zeros((1, 2, 15, 256), np.float32)
    mem_k_p = np.zeros((1, 2, 256, 4, 64), np.float32)
    mem_v_p = np.zeros((1, 2, 256, 4, 64), np.float32)
    sb_k_s = np.zeros((1, 16, 64, 8, 64), np.float32)
    sb_v_s = np.zeros((1, 16, 64, 8, 64), np.float32)
    pool_s = np.zeros((1, 16, 15, 256), np.float32)
    for c in range(NCORES):
        b, j = c // 4, c % 4
        r = R[c]
        yo = np.asarray(r["y_out"])
        y_prompt[b, 2048 * j:2048 * j + 2048] = yo[0:2048]
        y_sample[8 * b + 2 * j:8 * b + 2 * j + 2] = yo[2048:2176].reshape(2, 64, D)
        ko, vo = np.asarray(r["k_out"]), np.asarray(r["v_out"])
        sb_k_p[0, b, :, 2 * j:2 * j + 2, :] = ko[0:SEQ].reshape(SEQ, 2, 64)
        sb_v_p[0, b, :, 2 * j:2 * j + 2, :] = vo[0:SEQ].reshape(SEQ, 2, 64)
        sb_k_s[0, 8 * b:8 * b + 8, :, 2 * j:2 * j + 2, :] = ko[SEQ:].reshape(8, 64, 2, 64)
        sb_v_s[0, 8 * b:8 * b + 8, :, 2 * j:2 * j + 2, :] = vo[SEQ:].reshape(8, 64, 2, 64)
        uo = np.asarray(r["u_out"])
        pool_p[0, b, :, 64 * j:64 * j + 64] = uo[:, 0:15].T
        pool_s[0, 8 * b:8 * b + 8, :, 64 * j:64 * j + 64] = uo[:, 15:135].reshape(64, 8, 15).transpose(1, 2, 0)
        mo = np.asarray(r["memkv_out"])
        mem_k_p[0, b, :, j, :] = mo[:, 0:64]
        mem_v_p[0, b, :, j, :] = mo[:, 64:128]
    return (y_prompt, y_sample, sb_k_p, sb_v_p, pool_p, mem_k_p, mem_v_p, sb_k_s, sb_v_s, pool_s)
```
